# Optimizing a Trainium2 kernel written in Bass

```python
import jax, jax.numpy as jnp
from jax import lax
import numpy as np

D_MODEL = 1024
BATCH = 16
SEQ = 2048
DEPTH = 4

A_HEADS = 4
A_DK = 128
A_DV = 128
A_WIDTH = A_HEADS * A_DV
B_HEADS = 4
B_DK = 64
B_DV = 128
B_QK = B_HEADS * B_DK
B_WIDTH = B_HEADS * B_DV
C_HEADS = 4
C_DK = 64
C_DV = 128
C_QK = C_HEADS * C_DK
C_WIDTH = C_HEADS * C_DV
GLA_RANK = 16
GLA_TAU = 16.0
N_BRANCH = 3
SPLIT_SIZES = (
    A_WIDTH, A_HEADS * A_DK, A_HEADS * A_DK, A_WIDTH, A_WIDTH,
    B_QK, B_QK, B_WIDTH, B_WIDTH,
    C_QK, C_QK, C_WIDTH, C_WIDTH,
    2 * GLA_RANK,
    D_MODEL, D_MODEL, D_MODEL,
)
N_IN = 5 * A_WIDTH + 2 * B_QK + 2 * B_WIDTH + 2 * C_QK + 2 * C_WIDTH + 2 * GLA_RANK + N_BRANCH * D_MODEL
D_FF = ((8 * D_MODEL + 3 * 256 - 1) // (3 * 256)) * 256
CHUNK = 64
ROPE_BASE = 10000.0
EPS = 1e-6
TINY = 1e-30

kernel_name = "hybrid_hgrn2_retnet_gla_adaln_encoder"


def rms_norm(x, gain):
    x32 = x.astype(jnp.float32)
    y = x32 * lax.rsqrt(jnp.mean(x32 * x32, axis=-1, keepdims=True) + EPS)
    return (y * gain.astype(jnp.float32)).astype(x.dtype)


def head_rms(o):
    return o * lax.rsqrt(jnp.mean(o * o, axis=-1, keepdims=True) + EPS)


def head_group_norm(o):
    mu = jnp.mean(o, axis=-1, keepdims=True)
    var = jnp.mean(jnp.square(o - mu), axis=-1, keepdims=True)
    return (o - mu) * lax.rsqrt(var + EPS)


def rotary(x):
    seqlen, d = x.shape[1], x.shape[-1]
    pos = jnp.arange(seqlen, dtype=jnp.float32)
    inv_freq = ROPE_BASE ** (-jnp.arange(0, d, 2, dtype=jnp.float32) / d)
    ang = pos[:, None] * inv_freq[None, :]
    cos = jnp.cos(ang)[None, :, None, :]
    sin = jnp.sin(ang)[None, :, None, :]
    x32 = x.astype(jnp.float32)
    x1, x2 = x32[..., : d // 2], x32[..., d // 2:]
    return jnp.concatenate([x1 * cos - x2 * sin, x1 * sin + x2 * cos], axis=-1).astype(x.dtype)


def chunked_gated_linear_attention(q, k, v, log_g, inclusive):
    bsz, seqlen, heads, dk = q.shape
    dv = v.shape[-1]
    n = seqlen // CHUNK

    def blocks(a):
        a = a.astype(jnp.float32).reshape(bsz, n, CHUNK, heads, a.shape[-1])
        return a.transpose(1, 0, 3, 2, 4)

    mask = jnp.tril(jnp.ones((CHUNK, CHUNK), dtype=bool), k=0 if inclusive else -1)[:, :, None]

    def step(state, inp):
        qb, kb, vb, gb = inp
        cum = jnp.cumsum(gb, axis=2)
        last = cum[:, :, -1:, :]
        diff = jnp.where(mask, cum[:, :, :, None, :] - cum[:, :, None, :, :], 0.0)
        rel = jnp.where(mask, jnp.exp(diff), 0.0)
        scores = jnp.einsum('bhid,bhjd,bhijd->bhij', qb, kb, rel)
        out = (jnp.einsum('bhij,bhjv->bhiv', scores, vb)
               + jnp.einsum('bhid,bhdv->bhiv', qb * jnp.exp(cum), state))
        state = (state * jnp.exp(last[:, :, 0, :, None])
                 + jnp.einsum('bhjd,bhjv->bhdv', kb * jnp.exp(last - cum), vb))
        return state, out

    init = jnp.zeros((bsz, heads, dk, dv), jnp.float32)
    _, out = lax.scan(step, init, (blocks(q), blocks(k), blocks(v), blocks(log_g)))
    return out.transpose(1, 0, 3, 2, 4).reshape(bsz, seqlen, heads, dv)


def bidirectional_scan(q, k_fwd, k_bwd, v, log_g_fwd, log_g_bwd):
    fwd = chunked_gated_linear_attention(q, k_fwd, v, log_g_fwd, True)
    flip = lambda a: jnp.flip(a, axis=1)
    bwd = flip(chunked_gated_linear_attention(flip(q), flip(k_bwd), flip(v), flip(log_g_bwd), False))
    return fwd + bwd


def retention_log_decay(reverse):
    h = jnp.arange(B_HEADS, dtype=jnp.float32)
    if reverse:
        h = h[::-1]
    return jnp.log1p(-jnp.exp2(-5.0 - h))


def hgrn2_lower_bounds(lb_logits):
    p = jax.nn.softmax(lb_logits.astype(jnp.float32), axis=0)
    return jnp.maximum(jnp.cumsum(p, axis=0) - p[0:1], 0.0)


def mixer_sublayer(h, lb, norm_a_g, norm_b_g, norm_c_g, w_in, w_alpha, b_alpha, w_pa, w_pb, w_pc, w_out):
    bsz, seqlen, _ = h.shape
    dt = h.dtype
    z = h @ w_in
    idx = np.cumsum(SPLIT_SIZES)[:-1]
    (a_q, a_ff, a_fb, a_i, a_g, b_q, b_k, b_v, b_g, c_q, c_k, c_v, c_g, c_lr,
     gate_a, gate_b, gate_c) = jnp.split(z, idx, axis=-1)
    heads = lambda t, n: t.reshape(bsz, seqlen, n, -1)

    lb32 = lb.astype(jnp.float32).reshape(A_HEADS, A_DK)
    def forget(zf):
        zf = heads(zf, A_HEADS).astype(jnp.float32)
        f = lb32 + (1.0 - lb32) * jax.nn.sigmoid(zf)
        log_f = jnp.log(jnp.maximum(f, TINY))
        key = (1.0 - lb32) * jax.nn.sigmoid(-zf)
        return log_f, key
    log_f_fwd, key_fwd = forget(a_ff)
    log_f_bwd, key_bwd = forget(a_fb)
    qa = heads(jax.nn.silu(a_q), A_HEADS)
    o_a = bidirectional_scan(qa, key_fwd, key_bwd, heads(a_i, A_HEADS), log_f_fwd, log_f_bwd)
    y_a = (head_rms(o_a).reshape(bsz, seqlen, A_WIDTH) * norm_a_g).astype(dt) * jax.nn.sigmoid(a_g)

    qb = rotary(heads(b_q, B_HEADS))
    kb = rotary(heads(b_k, B_HEADS)) * (B_DK ** -0.5)
    shape_b = (bsz, seqlen, B_HEADS, B_DK)
    lg_fwd = jnp.broadcast_to(retention_log_decay(False)[None, None, :, None], shape_b)
    lg_bwd = jnp.broadcast_to(retention_log_decay(True)[None, None, :, None], shape_b)
    o_b = bidirectional_scan(qb, kb, kb, heads(b_v, B_HEADS), lg_fwd, lg_bwd)
    y_b = (head_group_norm(o_b).reshape(bsz, seqlen, B_WIDTH) * norm_b_g).astype(dt) * jax.nn.silu(b_g)

    lr = c_lr.reshape(bsz, seqlen, 2, GLA_RANK)
    alpha_logits = jnp.einsum('btnr,nrk->btnk', lr, w_alpha) + b_alpha
    log_alpha = jax.nn.log_sigmoid(alpha_logits.astype(jnp.float32)) / GLA_TAU
    la_fwd = heads(log_alpha[:, :, 0], C_HEADS)
    la_bwd = heads(log_alpha[:, :, 1], C_HEADS)
    qc = heads(c_q, C_HEADS) * (C_DK ** -0.5)
    kc = heads(c_k, C_HEADS)
    o_c = bidirectional_scan(qc, kc, kc, heads(c_v, C_HEADS), la_fwd, la_bwd)
    y_c = (head_rms(o_c).reshape(bsz, seqlen, C_WIDTH) * norm_c_g).astype(dt) * jax.nn.silu(c_g)

    merged = (jax.nn.sigmoid(gate_a) * (y_a @ w_pa)
              + jax.nn.sigmoid(gate_b) * (y_b @ w_pb)
              + jax.nn.sigmoid(gate_c) * (y_c @ w_pc))
    return merged @ w_out


def setup_inputs(seed: int = 0) -> dict:
    key = jax.random.key(seed)
    ks = jax.random.split(key, 24)
    f32 = jnp.float32
    nrm = lambda k, shape, scale: jax.random.normal(k, shape, f32) * scale
    gain = lambda k, shape: 1.0 + 0.01 * jax.random.normal(k, shape, f32)
    return {
        "x": jax.random.normal(ks[0], (BATCH, SEQ, D_MODEL), f32),
        "c": jax.random.normal(ks[1], (BATCH, D_MODEL), f32),
        "norm1_g": gain(ks[2], (DEPTH, D_MODEL)),
        "w_ada": nrm(ks[3], (DEPTH, D_MODEL, 6 * D_MODEL), D_MODEL ** -0.5),
        "b_ada": nrm(ks[4], (DEPTH, 6 * D_MODEL), 0.01),
        "w_in": nrm(ks[5], (DEPTH, D_MODEL, N_IN), D_MODEL ** -0.5),
        "lb_logits": nrm(ks[6], (DEPTH, A_HEADS * A_DK), 0.5),
        "norm_a_g": gain(ks[7], (DEPTH, A_WIDTH)),
        "norm_b_g": gain(ks[8], (DEPTH, B_WIDTH)),
        "norm_c_g": gain(ks[9], (DEPTH, C_WIDTH)),
        "w_alpha": nrm(ks[10], (DEPTH, 2, GLA_RANK, C_QK), GLA_RANK ** -0.5),
        "b_alpha": nrm(ks[11], (DEPTH, 2, C_QK), 0.1),
        "w_pa": nrm(ks[12], (DEPTH, A_WIDTH, D_MODEL), A_WIDTH ** -0.5),
        "w_pb": nrm(ks[13], (DEPTH, B_WIDTH, D_MODEL), B_WIDTH ** -0.5),
        "w_pc": nrm(ks[14], (DEPTH, C_WIDTH, D_MODEL), C_WIDTH ** -0.5),
        "w_out": nrm(ks[15], (DEPTH, D_MODEL, D_MODEL), D_MODEL ** -0.5),
        "norm2_g": gain(ks[16], (DEPTH, D_MODEL)),
        "w_ffn_in": nrm(ks[17], (DEPTH, D_MODEL, 2 * D_FF), D_MODEL ** -0.5),
        "w_ffn_out": nrm(ks[18], (DEPTH, D_FF, D_MODEL), D_FF ** -0.5),
        "norm_f_g": gain(ks[19], (D_MODEL,)),
    }


def reference(x, c, norm1_g, w_ada, b_ada, w_in, lb_logits, norm_a_g, norm_b_g, norm_c_g,
              w_alpha, b_alpha, w_pa, w_pb, w_pc, w_out, norm2_g, w_ffn_in, w_ffn_out, norm_f_g):
    lower_bounds = hgrn2_lower_bounds(lb_logits)
    c_act = jax.nn.silu(c)
    for l in range(DEPTH):
        mod = c_act @ w_ada[l] + b_ada[l]
        sh1, sc1, g1, sh2, sc2, g2 = [m[:, None, :] for m in jnp.split(mod, 6, axis=-1)]
        h = rms_norm(x, norm1_g[l]) * (1.0 + sc1) + sh1
        x = x + g1 * mixer_sublayer(h, lower_bounds[l], norm_a_g[l], norm_b_g[l], norm_c_g[l], w_in[l],
                                    w_alpha[l], b_alpha[l], w_pa[l], w_pb[l], w_pc[l], w_out[l])
        h2 = rms_norm(x, norm2_g[l]) * (1.0 + sc2) + sh2
        gate, up = jnp.split(h2 @ w_ffn_in[l], 2, axis=-1)
        x = x + g2 * ((jax.nn.silu(gate) * up) @ w_ffn_out[l])
    return rms_norm(x, norm_f_g)
```

```python
import numpy as np
from contextlib import ExitStack
import concourse.bass as bass
import concourse.mybir as mybir
from concourse.bass_utils import run_bass_kernel_spmd

F32 = mybir.dt.float32
BF16 = mybir.dt.bfloat16
AF = mybir.ActivationFunctionType
ALU = mybir.AluOpType

D = 1024
T = 2048
L = 4
NT = 16
NB = 4
KC = 8
NCORES = 8
DFF = 2816
NFC = 22
EPS = 1e-6

OFF = dict(a_q=0, a_ff=512, a_fb=1024, a_i=1536, a_g=2048, b_q=2560, b_k=2816, b_v=3072, b_g=3584,
           c_q=4096, c_k=4352, c_v=4608, c_g=5120, c_lr=5632, gate_a=5664, gate_b=6688, gate_c=7712)


def _zgroup_cols():
    g = {}
    ar = np.arange(128)
    for h in range(4):
        g[f"QA{h}"] = OFF["a_q"] + h * 128 + ar
        g[f"FF{h}"] = OFF["a_ff"] + h * 128 + ar
        g[f"FB{h}"] = OFF["a_fb"] + h * 128 + ar
        g[f"VA{h}"] = OFF["a_i"] + h * 128 + ar
        g[f"GA{h}"] = OFF["a_g"] + h * 128 + ar
        g[f"VB{h}"] = OFF["b_v"] + h * 128 + ar
        g[f"GB{h}"] = OFF["b_g"] + h * 128 + ar
        g[f"VC{h}"] = OFF["c_v"] + h * 128 + ar
        g[f"GC{h}"] = OFF["c_g"] + h * 128 + ar
    sw = (ar // 64) * 64 + ((ar % 64) + 32) % 64
    for P in range(2):
        g[f"QB{P}"] = OFF["b_q"] + P * 128 + ar
        g[f"QBs{P}"] = OFF["b_q"] + P * 128 + sw
        g[f"KB{P}"] = OFF["b_k"] + P * 128 + ar
        g[f"KBs{P}"] = OFF["b_k"] + P * 128 + sw
        g[f"QC{P}"] = OFF["c_q"] + P * 128 + ar
        g[f"KC{P}"] = OFF["c_k"] + P * 128 + ar
    g["LR"] = OFF["c_lr"] + (ar % 32)
    for oc in range(8):
        g[f"GMa{oc}"] = OFF["gate_a"] + oc * 128 + ar
        g[f"GMb{oc}"] = OFF["gate_b"] + oc * 128 + ar
        g[f"GMc{oc}"] = OFF["gate_c"] + oc * 128 + ar
    return g


ZG = _zgroup_cols()
ZNAMES = list(ZG.keys())
ZIDX = {n: i for i, n in enumerate(ZNAMES)}
NZG = len(ZNAMES)

SMO = {}
_o = 0
for _n, _w in [("c", 16), ("bada", L * 48), ("n1g", L * 8), ("n2g", L * 8), ("nfg", 8), ("nag", L * 4),
               ("nbg", L * 4), ("ncg", L * 4), ("lbl", L * 4), ("bal", L * 4), ("bmask", 8)]:
    SMO[_n] = _o
    _o += _w
NSM = _o


class Eng:
    def __init__(self, name, h, sem):
        self.key = name
        self.h = h
        self.sem = sem
        self.cnt = 0
        self.seen = {}


class DSem:
    def __init__(self, name, sem):
        self.key = name
        self.sem = sem
        self.cnt = 0


class Buf:
    __slots__ = ("w", "r", "name")

    def __init__(self, name=""):
        self.w = {}
        self.r = {}
        self.name = name


class Builder:
    def __init__(self, layers, final, dbg=None):
        self.layers = layers
        self.final = final
        self.dbg = dbg
        self.nc = bass.Bass("TRN2", target_bir_lowering=False)
        self.es = ExitStack()
        nc = self.nc
        self.PE = Eng("pe", nc.tensor, self.sem("pe"))
        self.ACT = Eng("act", nc.scalar, self.sem("act"))
        self.DVE = Eng("dve", nc.vector, self.sem("dve"))
        self.SP = Eng("sp", nc.sync, None)
        self.POOL = Eng("pool", nc.gpsimd, None)
        self.dsems = []
        self.bank_i = 0
        self.dbg_on = False
        self.dbg_off = 0
        self.dbg_map = {}

    def sem(self, name):
        return self.es.enter_context(self.nc.semaphore(name))

    def dsem(self, name):
        d = DSem(name, self.sem(name))
        self.dsems.append(d)
        return d

    def sb(self, es, name, shape, dt):
        self.uid = getattr(self, "uid", 0) + 1
        return es.enter_context(self.nc.sbuf_tensor(f"{name}_{self.uid}", shape, dt))

    def _sync(self, eng, reads, writes):
        need = {}
        for b in reads:
            for k, ev in b.w.items():
                if k not in need or need[k][1] < ev[1]:
                    need[k] = ev
        for b in writes:
            for dct in (b.w, b.r):
                for k, ev in dct.items():
                    if k not in need or need[k][1] < ev[1]:
                        need[k] = ev
        for k, ev in need.items():
            sem, v = ev[0], ev[1]
            if k == eng.key and (eng is self.PE or eng.cnt - v >= 3 or ev[2] >= 64):
                continue
            if eng.seen.get(k, 0) >= v:
                continue
            eng.h.wait_ge(sem, v)
            eng.seen[k] = v

    def op(self, eng, fn, reads=(), writes=(), nfree=4096):
        self._sync(eng, reads, writes)
        ins = fn()
        eng.cnt += 1
        ins.then_inc(eng.sem, 1)
        ev = (eng.sem, eng.cnt, nfree)
        for b in reads:
            b.r[eng.key] = ev
        for b in writes:
            b.w[eng.key] = ev
            b.r = {}

    def dma_multi(self, q, items, ds):
        for (_, _, reads, writes) in items:
            self._sync(q, reads, writes)
        for (o, i, _, _) in items:
            q.h.dma_start(out=o, in_=i).then_inc(ds.sem, 16)
            ds.cnt += 16
        ev = (ds.sem, ds.cnt, 4096)
        for (_, _, reads, writes) in items:
            for b in reads:
                b.r[ds.key] = ev
            for b in writes:
                b.w[ds.key] = ev
                b.r = {}

    def dma(self, q, out, in_, ds, reads=(), writes=()):
        self.dma_multi(q, [(out, in_, reads, writes)], ds)

    def barrier(self):
        engs = [self.PE, self.ACT, self.DVE]
        for e in engs + [self.SP]:
            for o in engs:
                if o is e or o.cnt == 0:
                    continue
                if e.seen.get(o.key, 0) >= o.cnt:
                    continue
                e.h.wait_ge(o.sem, o.cnt)
                e.seen[o.key] = o.cnt
            for d in self.dsems:
                if d.cnt == 0 or e.seen.get(d.key, 0) >= d.cnt:
                    continue
                e.h.wait_ge(d.sem, d.cnt)
                e.seen[d.key] = d.cnt

    def mm(self, out, lhsT, rhs, start, stop, reads, writes):
        nc = self.nc
        self.op(self.PE, lambda: nc.tensor.matmul(out, lhsT=lhsT, rhs=rhs, start=start, stop=stop), reads, writes)

    def act(self, out, in_, func, reads, writes, bias=None, scale=None):
        nc = self.nc
        kw = {}
        if bias is not None:
            kw["bias"] = bias
        if scale is not None:
            kw["scale"] = scale
        self.op(self.ACT, lambda: nc.scalar.activation(out=out, in_=in_, func=func, **kw), reads, writes, nfree=out.free_size())

    def tt(self, out, in0, in1, opx, reads, writes):
        nc = self.nc
        self.op(self.DVE, lambda: nc.vector.tensor_tensor(out=out, in0=in0, in1=in1, op=opx), reads, writes, nfree=out.free_size())

    def ts(self, out, in0, s1, s2, op0, op1, reads, writes):
        nc = self.nc
        if op1 is None:
            self.op(self.DVE, lambda: nc.vector.tensor_scalar(out=out, in0=in0, scalar1=s1, scalar2=None, op0=op0), reads, writes, nfree=out.free_size())
        else:
            self.op(self.DVE, lambda: nc.vector.tensor_scalar(out=out, in0=in0, scalar1=s1, scalar2=s2, op0=op0, op1=op1), reads, writes, nfree=out.free_size())

    def stt(self, out, in0, scalar, in1, op0, op1, reads, writes):
        nc = self.nc
        self.op(self.DVE, lambda: nc.vector.scalar_tensor_tensor(out=out, in0=in0, scalar=scalar, in1=in1, op0=op0, op1=op1), reads, writes, nfree=out.free_size())

    def vcopy(self, out, in_, reads, writes):
        nc = self.nc
        self.op(self.DVE, lambda: nc.vector.tensor_copy(out=out, in_=in_), reads, writes, nfree=out.free_size())

    def memset(self, ap, val, writes):
        nc = self.nc
        self.op(self.DVE, lambda: nc.vector.memset(ap, val), (), writes, nfree=ap.free_size())

    def dump(self, name, ap, reads, ncols):
        if not self.dbg or not self.dbg_on:
            return
        o = self.dbg_off
        self.dbg_off += ncols
        assert self.dbg_off <= self.dbg
        self.dbg_map[name] = (o, ncols)
        self.dma(self.POOL, self.d_dbg[:, o:o + ncols], ap, self.ds_dbg, reads, ())

    def next_bank(self):
        i = self.bank_i
        self.bank_i = (i + 1) % len(self.banks)
        return self.banks[i], self.bank_b[i]

    def load_w(self, src_ap, nk=8):
        i = self.w_i
        self.w_i = (i + 1) % len(self.wslots)
        t, b, ds = self.wslots[i], self.wslot_b[i], self.wslot_ds[i]
        self.dma(self.POOL, t[:, 0:nk, :], src_ap, ds, (), (b,))
        return t, b

    def build(self):
        nc = self.nc
        es = self.es
        layers = self.layers
        NL = len(layers)
        dr = lambda name, shape, dt=F32, kind="ExternalInput": nc.dram_tensor(name, shape, dt, kind=kind).ap()
        self.d_xT = dr("xT", [2, 128, KC, T])
        self.d_sm = dr("smalls", [128, NSM])
        self.d_wal = dr("wal", [16, L, 2, 256])
        self.d_cmat = dr("cmat", [128, 6, 128])
        self.d_rot = dr("rot", [2, 128, T])
        self.d_cumB = dr("cumB", [2, 2, 128, T + 1])
        self.d_wada = dr("wada", [NL, 16, 128, KC, 384])
        self.d_wz = dr("wz", [NL, NZG, 128, KC, 128])
        self.d_wp = dr("wp", [NL, 3, 8, 128, 4, 128])
        self.d_wo = dr("wo", [NL, 8, 128, KC, 128])
        self.d_wf1 = dr("wf1", [NL, NFC, 2, 128, KC, 128])
        self.d_wf2 = dr("wf2", [NL, 2, 8, 128, 11, 128])
        self.d_out = dr("yT", [2, 128, KC, T], F32, "ExternalOutput")
        self.d_yscr = dr("yscr", [12, 128, T], BF16, "Internal")
        if self.dbg:
            self.d_dbg = dr("dbg", [128, self.dbg], F32, "ExternalOutput")

        P = lambda name, shape, dt: self.sb(es, name, shape, dt)
        self.xT = P("xT_sb", [128, KC, T], F32)
        self.xTb = [[Buf() for _ in range(NB)] for _ in range(KC)]
        self.hT = P("hT_sb", [128, KC, T], BF16)
        self.hTb = [[Buf() for _ in range(NB)] for _ in range(KC)]
        self.sm = P("sm_sb", [128, NSM], F32)
        self.smb = Buf()
        self.cm = P("cm_sb", [128, 6, 128], BF16)
        self.cmb = Buf()
        self.modT = P("modT", [128, NL, 48, 2], F32)
        self.modb = Buf()
        self.scp = P("scp", [128, NL, 2, 8, 2], F32)
        self.lbv = P("lbv", [128, L, 4], F32)
        self.omlv = P("omlv", [128, L, 4], F32)
        self.nbal = P("nbal", [128, L * 4], F32)
        self.epsc = P("epsc", [128, 1], F32)
        self.onec = P("onec", [128, 1], F32)
        NWS = 5
        self.wslots = [P(f"ws{i}", [128, KC, 128], BF16) for i in range(NWS)]
        self.wslot_b = [Buf() for _ in range(NWS)]
        self.wslot_ds = [self.dsem(f"wsd{i}") for i in range(NWS)]
        self.w_i = 0
        self.banks = [es.enter_context(nc.psum_tensor(f"pb{i}", [128, 512], F32)) for i in range(7)]
        self.bank_b = [Buf() for _ in range(7)]
        self.pst = es.enter_context(nc.psum_tensor("pst", [128, 1024], BF16))
        self.pst_b = [Buf() for _ in range(8)]
        self.pst_i = 0
        self.ds_const = self.dsem("const")
        self.ds_constp = self.dsem("constp")
        self.ds_w2 = [self.dsem("w2a"), self.dsem("w2b")]
        self.ds_xin = self.dsem("xin")
        self.ds_tab = self.dsem("tab")
        self.ds_cum = self.dsem("cum")
        self.ds_cum2 = self.dsem("walsem")
        self.ds_dbg = self.dsem("dbgsem")
        self.ds_y = self.dsem("yst")
        self.ds_yld = self.dsem("yld")
        self.ds_out = [self.dsem("out0"), self.dsem("out1")]
        self.ds_wada = [self.dsem("wada0"), self.dsem("wada1")]
        self.yscr_b = [Buf() for _ in range(12)]

        self.prologue()
        for s in range(2):
            self.load_x(s)
            for li, l in enumerate(layers):
                self.dbg_on = (s == 0 and li == 0)
                self.layer(s, li, l)
                self.dbg_on = False
            self.finish(s)
        for d in self.ds_out:
            if d.cnt:
                self.SP.h.wait_ge(d.sem, d.cnt)
        self.barrier()
        self.es.close()
        return nc

    def smv(self, name, w):
        o = SMO[name]
        return self.sm[:, o:o + w]

    def prologue(self):
        nc = self.nc
        self.dma_multi(self.SP, [(self.sm[:, :], self.d_sm[:, :], (), (self.smb,))], self.ds_const)
        self.dma(self.POOL, self.cm[:, :, :], self.d_cmat[:, :, :], self.ds_constp, (), (self.cmb,))
        self.ident = self.cm[:, 0, :]
        self.onesb = self.cm[:, 1, :]
        self.masks = {(1, 0): self.cm[:, 2, :], (1, 1): self.cm[:, 3, :], (8, 0): self.cm[:, 4, :], (8, 1): self.cm[:, 5, :]}
        smb = self.smb
        self.memset(self.epsc[:, :], EPS, (smb,))
        self.memset(self.onec[:, :], 1.0, (smb,))
        with ExitStack() as ps:
            cact = self.sb(ps, "cact", [128, 16], F32)
            ctmp = self.sb(ps, "ctmp", [128, 64], F32)
            wsl = [self.sb(ps, f"wada{i}", [128, KC, 384], F32) for i in range(2)]
            wslb = [Buf(), Buf()]
            cb = Buf()
            self.act(ctmp[:, 0:16], self.smv("c", 16), AF.Sigmoid, (smb,), (cb,))
            self.tt(cact[:, :], ctmp[:, 0:16], self.smv("c", 16), ALU.mult, (cb, smb), (cb,))
            e = ctmp[:, 16:32]
            self.act(e, self.smv("lbl", 16), AF.Exp, (smb,), (cb,))
            ssum = ctmp[:, 32:36]
            self.tt(ssum, e[:, 0:4], e[:, 4:8], ALU.add, (cb,), (cb,))
            self.tt(ssum, ssum, e[:, 8:12], ALU.add, (cb,), (cb,))
            self.tt(ssum, ssum, e[:, 12:16], ALU.add, (cb,), (cb,))
            rs = ctmp[:, 36:40]
            self.op(self.DVE, lambda: nc.vector.reciprocal(out=rs, in_=ssum), (cb,), (cb,), nfree=4)
            pr = ctmp[:, 40:56]
            for l in range(L):
                self.tt(pr[:, l * 4:(l + 1) * 4], e[:, l * 4:(l + 1) * 4], rs, ALU.mult, (cb,), (cb,))
            self.memset(self.lbv[:, 0, :], 0.0, (cb,))
            self.vcopy(self.lbv[:, 1, :], pr[:, 4:8], (cb,), (cb,))
            self.tt(self.lbv[:, 2, :], self.lbv[:, 1, :], pr[:, 8:12], ALU.add, (cb,), (cb,))
            self.tt(self.lbv[:, 3, :], self.lbv[:, 2, :], pr[:, 12:16], ALU.add, (cb,), (cb,))
            self.ts(self.omlv[:, :, :], self.lbv[:, :, :], -1.0, 1.0, ALU.mult, ALU.add, (cb,), (cb,))
            self.ts(self.nbal[:, :], self.smv("bal", L * 4), -1.0, None, ALU.mult, None, (smb,), (cb,))
            self.cb = cb
            for li, l in enumerate(self.layers):
                bank, bb = self.next_bank()
                for g in range(16):
                    i = g % 2
                    self.dma(self.SP, wsl[i][:, :, :], self.d_wada[li, g, :, :, :], self.ds_wada[i], (), (wslb[i],))
                    for mi in range(3):
                        m = g * 3 + mi
                        for kc in range(KC):
                            self.mm(bank[:, m * 2:(m + 1) * 2], wsl[i][:, kc, mi * 128:(mi + 1) * 128],
                                    cact[:, kc * 2:(kc + 1) * 2], kc == 0, kc == KC - 1, (wslb[i], cb), (bb,))
                o = SMO["bada"] + l * 48
                self.tt(self.modT[:, li, :, :], bank[:, 0:96].rearrange("p (m b) -> p m b", b=2),
                        self.sm[:, o:o + 48].unsqueeze(2).to_broadcast([128, 48, 2]), ALU.add, (bb, smb), (self.modb,))
                for sub in range(2):
                    sc = self.modT[:, li, 8 + 24 * sub:16 + 24 * sub, :]
                    og = SMO["n1g" if sub == 0 else "n2g"] + l * 8
                    self.stt(self.scp[:, li, sub, :, :], sc, 1.0, self.sm[:, og:og + 8].unsqueeze(2).to_broadcast([128, 8, 2]),
                             ALU.add, ALU.mult, (self.modb, smb), (self.modb,))
            self.barrier()

    def load_x(self, s):
        items = []
        for kc in range(KC):
            items.append((self.xT[:, kc, :], self.d_xT[s, :, kc, :], (), tuple(self.xTb[kc])))
        self.dma_multi(self.SP, items, self.ds_xin)

    def emit_norm(self, pes, scale_fn, shift_fn, sbufs):
        sq, sqb, rs_l, rsb_l, tf, tfb = sbufs
        for tb in range(NB):
            cs = slice(tb * 512, (tb + 1) * 512)
            bank, bb = self.next_bank()
            for kc in range(KC):
                i = kc % 2
                self.act(sq[i][:, :], self.xT[:, kc, cs], AF.Square, (self.xTb[kc][tb],), (sqb[i],))
                self.mm(bank[:, :], self.onesb, sq[i][:, :], kc == 0, kc == KC - 1, (sqb[i], self.cmb), (bb,))
            rs, rsb = rs_l[tb % 2], rsb_l[tb % 2]
            self.act(rs[:, :], bank[:, :], AF.Ln, (bb,), (rsb,), bias=self.epsc[:, 0:1], scale=1.0 / D)
            self.act(rs[:, :], rs[:, :], AF.Exp, (rsb,), (rsb,), scale=-0.5)
            if scale_fn is None:
                yield tb, rs, rsb
                continue
            for kc in range(KC):
                i = kc % 2
                self.stt(tf[i][:, :], self.xT[:, kc, cs], scale_fn(kc), rs[:, :], ALU.mult, ALU.mult,
                         (self.xTb[kc][tb], rsb, self.modb), (tfb[i],))
                if kc % 2 == 0:
                    self.act(self.hT[:, kc, cs], tf[i][:, :], AF.Identity, (tfb[i], self.modb), (self.hTb[kc][tb],),
                             bias=shift_fn(kc), scale=1.0)
                else:
                    self.ts(self.hT[:, kc, cs], tf[i][:, :], shift_fn(kc), None, ALU.add, None, (tfb[i], self.modb), (self.hTb[kc][tb],))

    def norm_to_h(self, es, li, sub, b):
        sq = [self.sb(es, f"nsq{sub}{i}", [128, 512], BF16) for i in range(2)]
        rs = [self.sb(es, f"nrs{sub}{i}", [128, 512], F32) for i in range(2)]
        tf = [self.sb(es, f"ntf{sub}{i}", [128, 512], F32) for i in range(2)]
        bufs = (sq, [Buf(), Buf()], rs, [Buf(), Buf()], tf, [Buf(), Buf()])
        scale_fn = lambda kc: self.scp[:, li, sub, kc, b:b + 1]
        shift_fn = lambda kc: self.modT[:, li, 24 * sub + kc, b:b + 1]
        for _ in self.emit_norm(es, scale_fn, shift_fn, bufs):
            pass

    def zgroup(self, li, name, evac):
        wt, wb = self.load_w(self.d_wz[li, ZIDX[name], :, :, :])
        for tb in range(NB):
            cs = slice(tb * 512, (tb + 1) * 512)
            bank, bb = self.next_bank()
            for kc in range(KC):
                self.mm(bank[:, :], wt[:, kc, :], self.hT[:, kc, cs], kc == 0, kc == KC - 1, (wb, self.hTb[kc][tb]), (bb,))
            evac(tb, cs, bank, bb)

    def vgroup(self, li, name, V, Vb, col):
        wt, wb = self.load_w(self.d_wz[li, ZIDX[name], :, :, :])
        for t4 in range(4):
            bank, bb = self.next_bank()
            for j in range(4):
                t = t4 * 4 + j
                for kc in range(KC):
                    self.mm(bank[:, j * 128:(j + 1) * 128], self.hT[:, kc, t * 128:(t + 1) * 128], wt[:, kc, :],
                            kc == 0, kc == KC - 1, (wb, self.hTb[kc][t // 4]), (bb,))
            self.act(V[:, t4 * 4:(t4 + 1) * 4, col:col + 128], bank[:, :].rearrange("p (j n) -> p j n", n=128), AF.Copy,
                     (bb,), (Vb,))

    def mixer_phase(self, s, li, l):
        nc = self.nc
        with ExitStack() as es:
            W = lambda name, shape, dt: self.sb(es, name, shape, dt)
            with ExitStack() as nes:
                self.norm_to_h(nes, li, 0, s)
                self.barrier()
            self.dump("hT0", self.hT[:, 0, :], tuple(self.hTb[0]), T)
            self.dump("lbv", self.lbv[:, :, :].rearrange("p l h -> p (l h)"), (self.cb,), 16)
            self.dump("omlv", self.omlv[:, :, :].rearrange("p l h -> p (l h)"), (self.cb,), 16)
            self.dump("lbl", self.smv("lbl", 16), (self.smb,), 16)
            m = {}
            m["q"] = W("m_q", [128, T], BF16)
            m["k"] = W("m_k", [128, T], BF16)
            m["CUM"] = W("m_cum", [128, T + 1], F32)
            m["tmp"] = W("m_tmp", [128, 4, 1024], F32)
            m["QH"] = W("m_QH", [128, T], BF16)
            m["KT"] = W("m_KT", [128, T], BF16)
            m["KH"] = W("m_KH", [128, T], BF16)
            m["V"] = W("m_V", [128, NT, 256], BF16)
            m["S"] = [W(f"m_S{i}", [128, 32 * 128], BF16) for i in range(2)]
            m["Sm"] = [W(f"m_Sm{i}", [128, 256], F32) for i in range(2)]
            m["KHt"] = [W(f"m_KHt{i}", [128, 1024], BF16) for i in range(2)]
            m["P"] = [W(f"m_P{i}", [128, 512], BF16) for i in range(2)]
            m["dec"] = W("m_dec", [128, 128], F32)
            m["dd"] = W("m_dd", [128, 128], F32)
            m["oT"] = W("m_oT", [128, 2, T], F32)
            m["wal"] = W("m_wal", [16, 512], BF16)
            m["gt"] = [m["tmp"][:, 0, i * 512:(i + 1) * 512] for i in range(2)]
            m["rs"] = m["tmp"][:, 1, 0:512]
            m["t1"] = m["tmp"][:, 1, 512:1024]
            sqv = m["tmp"][:, 2, :].bitcast(BF16)
            m["sq"] = [sqv[:, i * 512:(i + 1) * 512] for i in range(2)]
            m["yst"] = m["tmp"][:, 3, :].bitcast(BF16)
            B = {k: Buf(k) for k in ["q", "k", "CUM", "QH", "KT", "KH", "V", "dec", "dd", "wal"]}
            B["S"] = [Buf(), Buf()]
            B["tmp"] = [Buf() for _ in range(4)]
            B["rs"] = B["t1"] = B["tmp"][1]
            B["yst"] = B["tmp"][3]
            B["Sm"] = [Buf(), Buf()]
            B["KHt"] = [Buf(), Buf()]
            B["dS"] = [Buf(), Buf()]
            self.kh_i = 0
            self.p_i = 0
            B["P"] = [Buf(), Buf()]
            B["oT"] = [[Buf() for _ in range(NB)] for _ in range(2)]
            B["gt"] = [B["tmp"][0], B["tmp"][0]]
            B["sq"] = [B["tmp"][2], B["tmp"][2]]
            self.m, self.B = m, B
            self.dma(self.POOL, m["wal"][:, :], self.d_wal[:, l, :, :].rearrange("r n k -> r (n k)"), self.ds_cum2, (), (B["wal"],))
            self.memset(m["CUM"][:, 0:1], 0.0, (B["CUM"],))
            for h in range(4):
                self.mixer_A(s, li, l, h)
            for P in range(2):
                self.mixer_B(s, li, l, P)
            for P in range(2):
                self.mixer_C(s, li, l, P)
            self.barrier()

    def factors(self, d, nsub, cscale):
        m, B = self.m, self.B
        C = 128 // nsub
        nch = T // C
        CUM = m["CUM"]
        hc = nch // 2
        for hf in range(2):
            c0 = hf * hc
            tcol = slice(hf * 1024, (hf + 1) * 1024)
            Xv = CUM[:, 1 + hf * 1024:1 + (hf + 1) * 1024].rearrange("p (c k) -> p c k", k=C)
            Yv = CUM[:, hf * 1024:(hf + 1) * 1024].rearrange("p (c k) -> p c k", k=C)
            b0 = Yv[:, :, 0:1].to_broadcast([128, hc, C])
            b1 = Xv[:, :, C - 1:C].to_broadcast([128, hc, C])
            tD = m["tmp"][:, 0 + hf * 2, :]
            tE = m["tmp"][:, 1 + hf * 2, :]
            bD, bE = B["tmp"][0 + hf * 2], B["tmp"][1 + hf * 2]
            tDv = tD.rearrange("p (c k) -> p c k", k=C)
            if d == 0:
                self.tt(tDv, Xv, b0, ALU.subtract, (B["CUM"],), (bD,))
            else:
                self.tt(tDv, b1, Yv, ALU.subtract, (B["CUM"],), (bD,))
            self.act(tE, tD, AF.Exp, (bD,), (bE,), scale=cscale)
            self.tt(m["QH"][:, tcol], m["q"][:, tcol], tE, ALU.mult, (B["q"], bE), (B["QH"],))
            self.act(tE, tD, AF.Exp, (bD,), (bE,), scale=-cscale)
            self.tt(m["KT"][:, tcol], m["k"][:, tcol], tE, ALU.mult, (B["k"], bE), (B["KT"],))
            if d == 0:
                self.tt(tDv, b1, Xv, ALU.subtract, (B["CUM"],), (bD,))
            else:
                self.tt(tDv, Yv, b0, ALU.subtract, (B["CUM"],), (bD,))
            self.act(tE, tD, AF.Exp, (bD,), (bE,), scale=cscale)
            self.tt(m["KH"][:, tcol], m["k"][:, tcol], tE, ALU.mult, (B["k"], bE), (B["KH"],))
        e1 = CUM[:, 1:T + 1].rearrange("p (c k) -> p c k", k=C)[:, :, C - 1:C]
        e0 = CUM[:, 0:T].rearrange("p (c k) -> p c k", k=C)[:, :, 0:1]
        self.tt(m["dd"][:, 0:nch].unsqueeze(2), e1, e0, ALU.subtract, (B["CUM"],), (B["dd"],))
        self.act(m["dec"][:, 0:nch], m["dd"][:, 0:nch], AF.Exp, (B["dd"],), (B["dec"],), scale=cscale)

    def chain(self, d, nsub, Wd, qt, first):
        nc = self.nc
        m = self.m
        B = dict(self.B)
        B["S"] = self.B["S"][qt % 2]
        C = 128 // nsub
        nch = T // C
        qch = nch // 4
        S = m["S"][qt % 2]
        Sv = lambda c: S[:, (c - qt * qch) * Wd:(c - qt * qch + 1) * Wd]
        tiles = list(range(qt * 4, qt * 4 + 4))
        if d == 1:
            tiles = tiles[::-1]
        c_first = tiles[0] * nsub if d == 0 else tiles[0] * nsub + nsub - 1
        if first:
            self.cur = 0
            self.memset(m["Sm"][0][:, 0:Wd], 0.0, (B["Sm"][0],))
            self.memset(Sv(c_first), 0.0, (B["S"],))
        else:
            self.act(Sv(c_first), m["Sm"][self.cur][:, 0:Wd], AF.Copy, (B["Sm"][self.cur],), (B["S"],))
        bmask = self.smv("bmask", 8)

        def prep_tile(t):
            ts_ = slice(t * 128, (t + 1) * 128)
            pi = self.pst_i
            self.pst_i = (pi + 1) % 8
            pslot = self.pst[:, pi * 128:(pi + 1) * 128]
            pb = self.pst_b[pi]
            self.op(self.PE, lambda: nc.tensor.transpose(pslot, self.m["KH"][:, ts_], self.ident), (B["KH"], self.cmb), (pb,))
            ki = self.kh_i
            self.kh_i = 1 - ki
            kht, khb = m["KHt"][ki], B["KHt"][ki]
            if nsub == 1:
                self.vcopy(kht[:, 0:128], pslot, (pb,), (khb,))
            else:
                self.tt(kht[:, 0:nsub * 128].rearrange("p (a n) -> p a n", n=128), pslot.unsqueeze(1).to_broadcast([128, nsub, 128]),
                        bmask.unsqueeze(2).to_broadcast([128, nsub, 128]), ALU.mult, (pb, self.smb), (khb,))
            return kht, khb

        nxt_prep = prep_tile(tiles[0])
        for it, t in enumerate(tiles):
            kht, khb = nxt_prep
            if it + 1 < len(tiles):
                nxt_prep = prep_tile(tiles[it + 1])
            if nsub == 1:
                bank, bb = self.next_bank()
                self.mm(bank[:, 0:Wd], kht[:, 0:128], m["V"][:, t, 0:Wd], True, True, (khb, B["V"]), (bb,))
                dsl = [(bank[:, 0:Wd], bb)]
            else:
                dsl = []
                for g in range(nsub // 4):
                    bank, bb = self.next_bank()
                    for a4 in range(4):
                        a = g * 4 + a4
                        self.mm(bank[:, a4 * 128:(a4 + 1) * 128], kht[:, a * 128:(a + 1) * 128], m["V"][:, t, 0:128], True, True,
                                (khb, B["V"]), (bb,))
                        dsl.append((bank[:, a4 * 128:(a4 + 1) * 128], bb))
            subs = range(nsub) if d == 0 else range(nsub - 1, -1, -1)
            for a in subs:
                c = t * nsub + a
                nxt = c + 1 if d == 0 else c - 1
                if nxt < 0 or nxt >= nch:
                    continue
                cur = self.cur
                nx = 1 - cur
                dsa, dsb = dsl[a]
                inq = qt * qch <= nxt < (qt + 1) * qch
                if nsub > 1:
                    if inq:
                        self.stt(Sv(nxt), Sv(c), m["dec"][:, c:c + 1], dsa, ALU.mult, ALU.add, (B["S"], B["dec"], dsb), (B["S"],))
                    else:
                        self.stt(m["Sm"][nx][:, 0:Wd], Sv(c), m["dec"][:, c:c + 1], dsa, ALU.mult, ALU.add,
                                 (B["S"], B["dec"], dsb), (B["Sm"][nx],))
                        self.cur = nx
                    continue
                self.stt(m["Sm"][nx][:, 0:Wd], m["Sm"][cur][:, 0:Wd], m["dec"][:, c:c + 1], dsa,
                         ALU.mult, ALU.add, (B["Sm"][cur], B["dec"], dsb), (B["Sm"][nx],))
                if inq:
                    self.act(Sv(nxt), m["Sm"][nx][:, 0:Wd], AF.Copy, (B["Sm"][nx],), (B["S"],))
                self.cur = nx

    def outphase(self, d, nsub, Wd, hh, rows, vcol, qt):
        m = self.m
        B = dict(self.B)
        B["S"] = self.B["S"][qt % 2]
        C = 128 // nsub
        qch = (T // C) // 4
        mask = self.masks[(nsub, d)]
        S = m["S"][qt % 2]
        t4 = qt
        bankP, bPb = self.next_bank()
        for j in range(4):
            t = t4 * 4 + j
            ts_ = slice(t * 128, (t + 1) * 128)
            self.mm(bankP[:, j * 128:(j + 1) * 128], m["KT"][rows, ts_], m["QH"][rows, ts_], True, True,
                    (B["KT"], B["QH"]), (bPb,))
        pi = self.p_i
        self.p_i = 1 - pi
        Pt, Ptb = m["P"][pi], B["P"][pi]
        self.tt(Pt[:, :].rearrange("p (j n) -> p j n", n=128), bankP[:, :].rearrange("p (j n) -> p j n", n=128),
                mask.unsqueeze(1).to_broadcast([128, 4, 128]), ALU.mult, (bPb, self.cmb), (Ptb,))
        bankO, bOb = self.next_bank()
        for j in range(4):
            t = t4 * 4 + j
            self.mm(bankO[:, j * 128:(j + 1) * 128], m["V"][:, t, vcol:vcol + 128], Pt[:, j * 128:(j + 1) * 128], True, False,
                    (B["V"], Ptb), (bOb,))
            for a in range(nsub):
                c = t * nsub + a
                sl = c - qt * qch
                self.mm(bankO[:, j * 128 + a * C:j * 128 + (a + 1) * C],
                        S[rows, sl * Wd + hh * 128:sl * Wd + (hh + 1) * 128], m["QH"][rows, c * C:(c + 1) * C],
                        False, True, (B["S"], B["QH"]), (bOb,))
        osl = m["oT"][:, hh, t4 * 512:(t4 + 1) * 512]
        ob = B["oT"][hh][t4]
        if d == 0:
            self.act(osl, bankO[:, :], AF.Copy, (bOb,), (ob,))
        else:
            self.tt(osl, bankO[:, :], osl, ALU.add, (bOb, ob), (ob,))

    def scan_dir(self, d, nsub, Wd, heads):
        quarters = (0, 1, 2, 3) if d == 0 else (3, 2, 1, 0)
        self.chain(d, nsub, Wd, quarters[0], True)
        for iq, qt in enumerate(quarters):
            if iq + 1 < 4:
                self.chain(d, nsub, Wd, quarters[iq + 1], False)
            for (hh, rows, vcol) in heads:
                self.outphase(d, nsub, Wd, hh, rows, vcol, qt)

    @staticmethod
    def _merge_into(dst, srcs):
        for sb_ in srcs:
            for dct in (sb_.w, sb_.r):
                for k, ev in dct.items():
                    if k not in dst.w or dst.w[k][1] < ev[1]:
                        dst.w[k] = ev

    def yphase(self, s, li, hh, gate_name, gate_func, gain_ap, groupnorm, yidx):
        m, B = self.m, self.B
        tmp = m["tmp"]
        gt = [tmp[:, 0, 0:512], tmp[:, 0, 512:1024]]
        rs = [tmp[:, 1, 0:512], tmp[:, 1, 512:1024]]
        t1 = tmp[:, 2, 0:512]
        sqv = tmp[:, 2, 512:1024].bitcast(BF16)
        sq = [sqv[:, 0:512], sqv[:, 512:1024]]
        yst = m["yst"]
        f = {k: Buf(k) for k in ["gt0", "gt1", "rs0", "rs1", "t1", "sq0", "sq1"]}
        alias = {"gt0": 0, "gt1": 0, "rs0": 1, "rs1": 1, "t1": 2, "sq0": 2, "sq1": 2}
        for k, q in alias.items():
            self._merge_into(f[k], [B["tmp"][q]])
        gtb = [f["gt0"], f["gt1"]]
        rsb = [f["rs0"], f["rs1"]]
        sqb = [f["sq0"], f["sq1"]]
        t1b = f["t1"]
        wt, wb = self.load_w(self.d_wz[li, ZIDX[gate_name], :, :, :])
        for pr in range(2):
            blocks = [pr * 2, pr * 2 + 1]
            for i, tb in enumerate(blocks):
                cs = slice(tb * 512, (tb + 1) * 512)
                bank, bb = self.next_bank()
                for kc in range(KC):
                    self.mm(bank[:, :], wt[:, kc, :], self.hT[:, kc, cs], kc == 0, kc == KC - 1, (wb, self.hTb[kc][tb]), (bb,))
                self.act(gt[i], bank[:, :], gate_func, (bb,), (gtb[i],))
            banksN = []
            for i, tb in enumerate(blocks):
                cs = slice(tb * 512, (tb + 1) * 512)
                o = m["oT"][:, hh, cs]
                ob = B["oT"][hh][tb]
                if groupnorm:
                    self.vcopy(sq[i], o, (ob,), (sqb[i],))
                    bankM, bMb = self.next_bank()
                    self.mm(bankM[:, :], self.onesb, sq[i], True, True, (sqb[i], self.cmb), (bMb,))
                    self.stt(o, bankM[:, :], -1.0 / 128.0, o, ALU.mult, ALU.add, (bMb, ob), (ob,))
                self.tt(sq[i], o, o, ALU.mult, (ob,), (sqb[i],))
                bankN, bNb = self.next_bank()
                self.mm(bankN[:, :], self.onesb, sq[i], True, True, (sqb[i], self.cmb), (bNb,))
                banksN.append((bankN, bNb))
            for i, tb in enumerate(blocks):
                bankN, bNb = banksN[i]
                self.act(rs[i], bankN[:, :], AF.Ln, (bNb, self.smb), (rsb[i],), bias=self.epsc[:, 0:1], scale=1.0 / 128.0)
                self.act(rs[i], rs[i], AF.Exp, (rsb[i],), (rsb[i],), scale=-0.5)
            for i, tb in enumerate(blocks):
                cs = slice(tb * 512, (tb + 1) * 512)
                o = m["oT"][:, hh, cs]
                ob = B["oT"][hh][tb]
                self.stt(t1, o, gain_ap, rs[i], ALU.mult, ALU.mult, (ob, rsb[i], self.smb), (t1b,))
                self.tt(yst[:, cs], t1, gt[i], ALU.mult, (t1b, gtb[i]), (B["yst"],))
        self.dma(self.SP, self.d_yscr[yidx, :, :], yst[:, :], self.ds_y, (B["yst"],), (self.yscr_b[yidx],))
        for k, q in alias.items():
            self._merge_into(B["tmp"][q], [f[k]])

    def mixer_A(self, s, li, l, h):
        nc = self.nc
        m, B = self.m, self.B
        self.zgroup(li, f"QA{h}", lambda tb, cs, bank, bb: self.act(m["q"][:, cs], bank[:, :], AF.Silu, (bb,), (B["q"],)))
        self.vgroup(li, f"VA{h}", m["V"], B["V"], 0)
        if h == 0:
            self.dump("A_q", m["q"][:, :], (B["q"],), T)
        fs = m["oT"][:, 1, :]
        fb = B["oT"][1]
        lbc = self.lbv[:, l, h:h + 1]
        omc = self.omlv[:, l, h:h + 1]
        for d in range(2):
            self.zgroup(li, f"FF{h}" if d == 0 else f"FB{h}",
                        lambda tb, cs, bank, bb: self.act(fs[:, cs], bank[:, :], AF.Sigmoid, (bb,), (fb[tb],)))
            self.ts(fs, fs, omc, lbc, ALU.mult, ALU.add, tuple(fb) + (self.cb,), tuple(fb))
            self.ts(m["k"][:, :], fs, -1.0, 1.0, ALU.mult, ALU.add, tuple(fb), (B["k"],))
            self.act(fs, fs, AF.Ln, tuple(fb), tuple(fb))
            self.op(self.DVE, lambda: nc.vector.tensor_tensor_scan(out=m["CUM"][:, 1:T + 1], data0=self.onec[:, 0:1].to_broadcast([128, T]),
                                                                    data1=fs, initial=0.0, op0=ALU.mult, op1=ALU.add),
                    tuple(fb) + (self.smb,), (B["CUM"],))
            if h == 0:
                self.dump(f"A_k{d}", m["k"][:, :], (B["k"],), T)
                self.dump(f"A_cum{d}", m["CUM"][:, 1:T + 1], (B["CUM"],), T)
            self.factors(d, 8, 1.0)
            if h == 0:
                self.dump(f"A_QH{d}", m["QH"][:, :], (B["QH"],), T)
                self.dump(f"A_KT{d}", m["KT"][:, :], (B["KT"],), T)
                self.dump(f"A_KH{d}", m["KH"][:, :], (B["KH"],), T)
            self.scan_dir(d, 8, 128, [(0, slice(0, 128), 0)])
            if h == 0:
                self.dump(f"A_o{d}", m["oT"][:, 0, :], tuple(B["oT"][0]), T)
        o = SMO["nag"] + l * 4 + h
        self.yphase(s, li, 0, f"GA{h}", AF.Sigmoid, self.sm[:, o:o + 1], False, 0 + h)

    def rotary(self, li, dst, dstb, gname, gsname, scl):
        m, B = self.m, self.B
        ZA = m["tmp"][:, 0:2, :].rearrange("p a n -> p (a n)")
        ZB = m["tmp"][:, 2:4, :].rearrange("p a n -> p (a n)")
        bA = (B["tmp"][0], B["tmp"][1])
        bB = (B["tmp"][2], B["tmp"][3])
        self.zgroup(li, gname, lambda tb, cs, bank, bb: self.act(ZA[:, cs], bank[:, :], AF.Copy, (bb,), (bA[tb // 2],), scale=scl))
        self.zgroup(li, gsname, lambda tb, cs, bank, bb: self.act(ZB[:, cs], bank[:, :], AF.Copy, (bb,), (bB[tb // 2],), scale=scl))
        cosb = tuple(B["oT"][0])
        sinb = tuple(B["oT"][1])
        self.tt(ZA, ZA, m["oT"][:, 0, :], ALU.mult, bA + cosb, bA)
        self.tt(ZB, ZB, m["oT"][:, 1, :], ALU.mult, bB + sinb, bB)
        self.tt(dst[:, :], ZA, ZB, ALU.add, bA + bB, (dstb,))

    def mixer_B(self, s, li, l, P):
        m, B = self.m, self.B
        self.dma_multi(self.SP, [(m["oT"][:, 0, :], self.d_rot[0, :, :], (), tuple(B["oT"][0])),
                                 (m["oT"][:, 1, :], self.d_rot[1, :, :], (), tuple(B["oT"][1]))], self.ds_tab)
        self.rotary(li, m["q"], B["q"], f"QB{P}", f"QBs{P}", 1.0)
        self.rotary(li, m["k"], B["k"], f"KB{P}", f"KBs{P}", 0.125)
        self.vgroup(li, f"VB{2 * P}", m["V"], B["V"], 0)
        self.vgroup(li, f"VB{2 * P + 1}", m["V"], B["V"], 128)
        for d in range(2):
            self.dma(self.SP, m["CUM"][:, :], self.d_cumB[P, d, :, :], self.ds_cum, (), (B["CUM"],))
            self.factors(d, 1, 1.0)
            self.scan_dir(d, 1, 256, [(hh, slice(hh * 64, hh * 64 + 64), hh * 128) for hh in range(2)])
        for hh in range(2):
            h = 2 * P + hh
            o = SMO["nbg"] + l * 4 + h
            self.yphase(s, li, hh, f"GB{h}", AF.Silu, self.sm[:, o:o + 1], True, 4 + h)

    def mixer_C(self, s, li, l, P):
        nc = self.nc
        m, B = self.m, self.B
        self.zgroup(li, f"QC{P}", lambda tb, cs, bank, bb: self.act(m["q"][:, cs], bank[:, :], AF.Copy, (bb,), (B["q"],), scale=0.125))
        self.zgroup(li, f"KC{P}", lambda tb, cs, bank, bb: self.act(m["k"][:, cs], bank[:, :], AF.Copy, (bb,), (B["k"],)))
        self.vgroup(li, f"VC{2 * P}", m["V"], B["V"], 0)
        self.vgroup(li, f"VC{2 * P + 1}", m["V"], B["V"], 128)
        X = m["tmp"][:, 0:2, :].rearrange("p a n -> p (a n)")
        bX = (B["tmp"][0], B["tmp"][1])
        for d in range(2):
            wt, wb = self.load_w(self.d_wz[li, ZIDX["LR"], :, :, :])
            for tb in range(NB):
                cs = slice(tb * 512, (tb + 1) * 512)
                bank, bb = self.next_bank()
                for kc in range(KC):
                    self.mm(bank[0:16, :], wt[:, kc, d * 16:(d + 1) * 16], self.hT[:, kc, cs], kc == 0, kc == KC - 1,
                            (wb, self.hTb[kc][tb]), (bb,))
                self.act(m["KH"][0:16, cs], bank[0:16, :], AF.Copy, (bb,), (B["KH"],))
            wo = d * 256 + P * 128
            nb_ = self.nbal[:, l * 4 + d * 2 + P:l * 4 + d * 2 + P + 1]
            for tb in range(NB):
                cs = slice(tb * 512, (tb + 1) * 512)
                bank, bb = self.next_bank()
                self.mm(bank[:, :], m["wal"][0:16, wo:wo + 128], m["KH"][0:16, cs], True, True, (B["wal"], B["KH"]), (bb,))
                self.act(X[:, cs], bank[:, :], AF.Exp, (bb, self.cb), (bX[tb // 2],), bias=nb_, scale=-1.0)
            self.act(X, X, AF.Ln, bX + (self.smb,), bX, bias=self.onec[:, 0:1], scale=1.0)
            self.op(self.DVE, lambda: nc.vector.tensor_tensor_scan(out=m["CUM"][:, 1:T + 1], data0=self.onec[:, 0:1].to_broadcast([128, T]),
                                                                    data1=X, initial=0.0, op0=ALU.mult, op1=ALU.add),
                    bX + (self.smb,), (B["CUM"],))
            self.factors(d, 1, -1.0 / 16.0)
            self.scan_dir(d, 1, 256, [(hh, slice(hh * 64, hh * 64 + 64), hh * 128) for hh in range(2)])
        for hh in range(2):
            h = 2 * P + hh
            o = SMO["ncg"] + l * 4 + h
            self.yphase(s, li, hh, f"GC{h}", AF.Silu, self.sm[:, o:o + 1], False, 8 + h)

    def merge_phase(self, s, li, l):
        with ExitStack() as es:
            W = lambda name, shape, dt: self.sb(es, name, shape, dt)
            yall = W("g_yall", [128, 12, T], BF16)
            yb = Buf()
            mg = W("g_mg", [128, KC, T], BF16)
            mgb = [Buf() for _ in range(NB)]
            gs = [W(f"g_gs{i}", [128, 512], F32) for i in range(2)]
            gsb = [Buf() for _ in range(2)]
            acc = W("g_acc", [128, T], F32)
            accb = [Buf() for _ in range(NB)]
            items = [(yall[:, i, :], self.d_yscr[i, :, :], (self.yscr_b[i],), (yb,)) for i in range(12)]
            self.dma_multi(self.SP, items, self.ds_yld)
            for i in range(12):
                self.dump(f"y{i}", yall[:, i, :], (yb,), T)
            mnames = ["GMa", "GMb", "GMc"]
            for oc in range(8):
                for mi in range(3):
                    wg = self.load_w(self.d_wz[li, ZIDX[f"{mnames[mi]}{oc}"], :, :, :])
                    wp = self.load_w(self.d_wp[li, mi, oc, :, :, :], nk=4)
                    for tb in range(NB):
                        cs = slice(tb * 512, (tb + 1) * 512)
                        gi = tb % 2
                        bankG, bGb = self.next_bank()
                        for kc in range(KC):
                            self.mm(bankG[:, :], wg[0][:, kc, :], self.hT[:, kc, cs], kc == 0, kc == KC - 1,
                                    (wg[1], self.hTb[kc][tb]), (bGb,))
                        self.act(gs[gi][:, :], bankG[:, :], AF.Sigmoid, (bGb,), (gsb[gi],))
                        bankP, bPb = self.next_bank()
                        for kc in range(4):
                            self.mm(bankP[:, :], wp[0][:, kc, :], yall[:, mi * 4 + kc, cs], kc == 0, kc == 3,
                                    (wp[1], yb), (bPb,))
                        if mi == 0:
                            self.tt(acc[:, cs], bankP[:, :], gs[gi][:, :], ALU.mult, (bPb, gsb[gi]), (accb[tb],))
                        elif mi == 1:
                            self.tt(gs[gi][:, :], bankP[:, :], gs[gi][:, :], ALU.mult, (bPb, gsb[gi]), (gsb[gi],))
                            self.tt(acc[:, cs], acc[:, cs], gs[gi][:, :], ALU.add, (accb[tb], gsb[gi]), (accb[tb],))
                        else:
                            self.tt(gs[gi][:, :], bankP[:, :], gs[gi][:, :], ALU.mult, (bPb, gsb[gi]), (gsb[gi],))
                            self.tt(mg[:, oc, cs], acc[:, cs], gs[gi][:, :], ALU.add, (accb[tb], gsb[gi]), (mgb[tb],))
            self.dump("mg0", mg[:, 0, :], tuple(mgb), T)
            for oc in range(8):
                wt, wb = self.load_w(self.d_wo[li, oc, :, :, :])
                g1 = self.modT[:, li, 16 + oc, s:s + 1]
                for tb in range(NB):
                    cs = slice(tb * 512, (tb + 1) * 512)
                    bank, bb = self.next_bank()
                    for kc in range(KC):
                        self.mm(bank[:, :], wt[:, kc, :], mg[:, kc, cs], kc == 0, kc == KC - 1, (wb, mgb[tb]), (bb,))
                    xb = self.xTb[oc][tb]
                    self.stt(self.xT[:, oc, cs], bank[:, :], g1, self.xT[:, oc, cs], ALU.mult, ALU.add, (bb, xb, self.modb), (xb,))
            self.barrier()

    def ffn_phase(self, s, li, l):
        with ExitStack() as es:
            W = lambda name, shape, dt: self.sb(es, name, shape, dt)
            with ExitStack() as nes:
                self.norm_to_h(nes, li, 1, s)
                self.barrier()
            actT = W("f_act", [128, 11, T], BF16)
            ab = [Buf() for _ in range(NB)]
            st = [W(f"f_st{i}", [128, 512], F32) for i in range(2)]
            stb = [Buf(), Buf()]
            w2 = [W(f"f_w2{i}", [128, 11, 128], BF16) for i in range(2)]
            w2b = [Buf(), Buf()]
            w2ds = self.ds_w2
            for gi in range(2):
                for fcl in range(11):
                    fc = gi * 11 + fcl
                    wg, wgb = self.load_w(self.d_wf1[li, fc, 0, :, :, :])
                    wu, wub = self.load_w(self.d_wf1[li, fc, 1, :, :, :])
                    for tb in range(NB):
                        cs = slice(tb * 512, (tb + 1) * 512)
                        i = tb % 2
                        bankG, bGb = self.next_bank()
                        for kc in range(KC):
                            self.mm(bankG[:, :], wg[:, kc, :], self.hT[:, kc, cs], kc == 0, kc == KC - 1, (wgb, self.hTb[kc][tb]), (bGb,))
                        bankU, bUb = self.next_bank()
                        for kc in range(KC):
                            self.mm(bankU[:, :], wu[:, kc, :], self.hT[:, kc, cs], kc == 0, kc == KC - 1, (wub, self.hTb[kc][tb]), (bUb,))
                        self.act(st[i][:, :], bankG[:, :], AF.Silu, (bGb,), (stb[i],))
                        self.tt(actT[:, fcl, cs], bankU[:, :], st[i][:, :], ALU.mult, (bUb, stb[i]), (ab[tb],))
                for oc in range(8):
                    i = oc % 2
                    self.dma(self.POOL, w2[i][:, :, :], self.d_wf2[li, gi, oc, :, :, :], w2ds[i], (), (w2b[i],))
                    g2 = self.modT[:, li, 40 + oc, s:s + 1]
                    for tb in range(NB):
                        cs = slice(tb * 512, (tb + 1) * 512)
                        bank, bb = self.next_bank()
                        for fcl in range(11):
                            self.mm(bank[:, :], w2[i][:, fcl, :], actT[:, fcl, cs], fcl == 0, fcl == 10, (w2b[i], ab[tb]), (bb,))
                        xb = self.xTb[oc][tb]
                        self.stt(self.xT[:, oc, cs], bank[:, :], g2, self.xT[:, oc, cs], ALU.mult, ALU.add, (bb, xb, self.modb), (xb,))
            self.barrier()

    def layer(self, s, li, l):
        self.mixer_phase(s, li, l)
        self.merge_phase(s, li, l)
        self.ffn_phase(s, li, l)

    def finish(self, s):
        with ExitStack() as es:
            W = lambda name, shape, dt: self.sb(es, name, shape, dt)
            ost = [W(f"o_st{i}", [128, 512], F32) for i in range(2)]
            ostb = [Buf(), Buf()]
            k = 0
            if self.final:
                sq = [W(f"osq{i}", [128, 512], BF16) for i in range(2)]
                rs = [W(f"ors{i}", [128, 512], F32) for i in range(2)]
                bufs = (sq, [Buf(), Buf()], rs, [Buf(), Buf()], None, None)
                o = SMO["nfg"]
                for tb, rs_, rsb in self.emit_norm(es, None, None, bufs):
                    cs = slice(tb * 512, (tb + 1) * 512)
                    for kc in range(KC):
                        i = k % 2
                        k += 1
                        self.stt(ost[i][:, :], self.xT[:, kc, cs], self.sm[:, o + kc:o + kc + 1], rs_[:, :], ALU.mult, ALU.mult,
                                 (self.xTb[kc][tb], rsb, self.smb), (ostb[i],))
                        self.dma(self.SP, self.d_out[s, :, kc, cs], ost[i][:, :], self.ds_out[i], (ostb[i],), ())
            else:
                for kc in range(KC):
                    self.dma(self.SP, self.d_out[s, :, kc, :], self.xT[:, kc, :], self.ds_out[kc % 2], tuple(self.xTb[kc]), ())
            self.barrier()


def _tile_w(w, nk):
    n = w.shape[1]
    return np.ascontiguousarray(w.reshape(nk, 128, n).transpose(1, 0, 2))


def _const_tables():
    j = np.arange(128)[:, None]
    i = np.arange(128)[None, :]
    same = (j // 16) == (i // 16)
    cm = np.zeros((128, 6, 128), np.float32)
    cm[:, 0, :] = np.eye(128)
    cm[:, 1, :] = 1.0
    cm[:, 2, :] = (j <= i)
    cm[:, 3, :] = (j > i)
    cm[:, 4, :] = (j <= i) & same
    cm[:, 5, :] = (j > i) & same
    pos = np.arange(T, dtype=np.float32)
    inv_freq = (np.float32(10000.0) ** (-np.arange(0, 64, 2, dtype=np.float32) / np.float32(64))).astype(np.float32)
    ang = (pos[:, None] * inv_freq[None, :]).astype(np.float32)
    cos = np.cos(ang).astype(np.float32).T
    sin = np.sin(ang).astype(np.float32).T
    rot = np.zeros((2, 128, T), np.float32)
    for p in range(128):
        d = p % 64
        rot[0, p] = cos[d % 32]
        rot[1, p] = -sin[d % 32] if d < 32 else sin[d % 32]
    cumB = np.zeros((2, 2, 128, T + 1), np.float32)
    tt = np.arange(T + 1, dtype=np.float64)
    for P in range(2):
        for dd in range(2):
            for p in range(128):
                h = 2 * P + p // 64
                hx = h if dd == 0 else 3 - h
                lg = np.log1p(-np.exp2(-5.0 - hx))
                cumB[P, dd, p] = (tt * lg).astype(np.float32)
    return cm, rot, cumB


def _prep_layer_weights(inp, l):
    w_in = inp["w_in"][l]
    wz = np.stack([_tile_w(w_in[:, ZG[n]], 8) for n in ZNAMES], 0)
    wps = [inp["w_pa"][l], inp["w_pb"][l], inp["w_pc"][l]]
    wp = np.stack([np.stack([_tile_w(w[:, oc * 128:(oc + 1) * 128], 4) for oc in range(8)], 0) for w in wps], 0)
    wo = np.stack([_tile_w(inp["w_out"][l][:, oc * 128:(oc + 1) * 128], 8) for oc in range(8)], 0)
    wfi = inp["w_ffn_in"][l]
    wf1 = np.stack([np.stack([_tile_w(wfi[:, fc * 128:(fc + 1) * 128], 8),
                              _tile_w(wfi[:, DFF + fc * 128:DFF + (fc + 1) * 128], 8)], 0) for fc in range(NFC)], 0)
    wfo = inp["w_ffn_out"][l]
    wf2 = np.stack([np.stack([_tile_w(wfo[gi * 1408:(gi + 1) * 1408, oc * 128:(oc + 1) * 128], 11) for oc in range(8)], 0)
                    for gi in range(2)], 0)
    wada = np.stack([_tile_w(inp["w_ada"][l][:, g * 384:(g + 1) * 384], 8) for g in range(16)], 0)
    return dict(wz=wz, wp=wp, wo=wo, wf1=wf1, wf2=wf2, wada=wada)


def _smalls(inp, c2):
    sm = np.zeros((128, NSM), np.float32)
    col = lambda v, n: np.ascontiguousarray(v.reshape(n, 128).T)
    sm[:, SMO["c"]:SMO["c"] + 16] = np.stack([col(c2[b], 8) for b in range(2)], 2).reshape(128, 16)
    for l in range(L):
        sm[:, SMO["bada"] + l * 48:SMO["bada"] + (l + 1) * 48] = col(inp["b_ada"][l], 48)
        sm[:, SMO["n1g"] + l * 8:SMO["n1g"] + (l + 1) * 8] = col(inp["norm1_g"][l], 8)
        sm[:, SMO["n2g"] + l * 8:SMO["n2g"] + (l + 1) * 8] = col(inp["norm2_g"][l], 8)
        sm[:, SMO["nag"] + l * 4:SMO["nag"] + (l + 1) * 4] = col(inp["norm_a_g"][l], 4)
        sm[:, SMO["nbg"] + l * 4:SMO["nbg"] + (l + 1) * 4] = col(inp["norm_b_g"][l], 4)
        sm[:, SMO["ncg"] + l * 4:SMO["ncg"] + (l + 1) * 4] = col(inp["norm_c_g"][l], 4)
        sm[:, SMO["lbl"] + l * 4:SMO["lbl"] + (l + 1) * 4] = col(inp["lb_logits"][l], 4)
        for d in range(2):
            sm[:, SMO["bal"] + l * 4 + d * 2:SMO["bal"] + l * 4 + d * 2 + 2] = col(inp["b_alpha"][l, d], 2)
    sm[:, SMO["nfg"]:SMO["nfg"] + 8] = col(inp["norm_f_g"], 8)
    for a in range(8):
        sm[a * 16:(a + 1) * 16, SMO["bmask"] + a] = 1.0
    return sm


_CACHE = {}


def _get_nc(layers, final):
    key = (tuple(layers), final)
    if key not in _CACHE:
        _CACHE[key] = Builder(list(layers), final).build()
    return _CACHE[key]


def _run(inp, xT_cores, layers, final):
    nc = _get_nc(layers, final)
    cm, rot, cumB = _const_tables()
    lw = [_prep_layer_weights(inp, l) for l in layers]
    shared = {k: np.stack([w[k] for w in lw], 0) for k in lw[0]}
    wal = np.ascontiguousarray(inp["w_alpha"].transpose(2, 0, 1, 3))
    in_maps = []
    for core in range(NCORES):
        c2 = inp["c"][2 * core:2 * core + 2]
        mp = dict(xT=xT_cores[core], smalls=_smalls(inp, c2), wal=wal, cmat=cm, rot=rot, cumB=cumB)
        mp.update(shared)
        in_maps.append(mp)
    res = run_bass_kernel_spmd(nc, in_maps, core_ids=list(range(NCORES)))
    return [np.asarray(r["yT"]) for r in res.results]


def _to_xT(x):
    out = []
    for core in range(NCORES):
        xs = x[2 * core:2 * core + 2]
        out.append(np.ascontiguousarray(xs.transpose(0, 2, 1).reshape(2, KC, 128, T).transpose(0, 2, 1, 3)))
    return out


def _from_xT(outs):
    res = np.empty((16, T, D), np.float32)
    for core in range(NCORES):
        o = outs[core]
        res[2 * core:2 * core + 2] = o.transpose(0, 2, 1, 3).reshape(2, D, T).transpose(0, 2, 1)
    return res


FUSED = True


def kernel(**inputs):
    inp = {k: np.asarray(v, dtype=np.float32) for k, v in inputs.items()}
    xT = _to_xT(inp["x"])
    if FUSED:
        outs = _run(inp, xT, [0, 1, 2, 3], True)
    else:
        outs = xT
        for l in range(L):
            outs = _run(inp, outs, [l], l == L - 1)
    return _from_xT(outs)
```

```python
import numpy as np
from contextlib import ExitStack
import concourse.bass as bass
import concourse.mybir as mybir
from concourse.bass_utils import run_bass_kernel_spmd

F32 = mybir.dt.float32
BF16 = mybir.dt.bfloat16
AF = mybir.ActivationFunctionType
ALU = mybir.AluOpType

D = 1024
T = 2048
L = 4
NT = 16
NB = 4
KC = 8
NCORES = 8
DFF = 2816
NFC = 22
EPS = 1e-6

OFF = dict(a_q=0, a_ff=512, a_fb=1024, a_i=1536, a_g=2048, b_q=2560, b_k=2816, b_v=3072, b_g=3584,
           c_q=4096, c_k=4352, c_v=4608, c_g=5120, c_lr=5632, gate_a=5664, gate_b=6688, gate_c=7712)


def _zgroup_cols():
    g = {}
    ar = np.arange(128)
    for h in range(4):
        g[f"QA{h}"] = OFF["a_q"] + h * 128 + ar
        g[f"FF{h}"] = OFF["a_ff"] + h * 128 + ar
        g[f"FB{h}"] = OFF["a_fb"] + h * 128 + ar
        g[f"VA{h}"] = OFF["a_i"] + h * 128 + ar
        g[f"GA{h}"] = OFF["a_g"] + h * 128 + ar
        g[f"VB{h}"] = OFF["b_v"] + h * 128 + ar
        g[f"GB{h}"] = OFF["b_g"] + h * 128 + ar
        g[f"VC{h}"] = OFF["c_v"] + h * 128 + ar
        g[f"GC{h}"] = OFF["c_g"] + h * 128 + ar
    sw = (ar // 64) * 64 + ((ar % 64) + 32) % 64
    for P in range(2):
        g[f"QB{P}"] = OFF["b_q"] + P * 128 + ar
        g[f"QBs{P}"] = OFF["b_q"] + P * 128 + sw
        g[f"KB{P}"] = OFF["b_k"] + P * 128 + ar
        g[f"KBs{P}"] = OFF["b_k"] + P * 128 + sw
        g[f"QC{P}"] = OFF["c_q"] + P * 128 + ar
        g[f"KC{P}"] = OFF["c_k"] + P * 128 + ar
    g["LR"] = OFF["c_lr"] + (ar % 32)
    for oc in range(8):
        g[f"GMa{oc}"] = OFF["gate_a"] + oc * 128 + ar
        g[f"GMb{oc}"] = OFF["gate_b"] + oc * 128 + ar
        g[f"GMc{oc}"] = OFF["gate_c"] + oc * 128 + ar
    return g


ZG = _zgroup_cols()
ZNAMES = list(ZG.keys())
ZIDX = {n: i for i, n in enumerate(ZNAMES)}
NZG = len(ZNAMES)

SMO = {}
_o = 0
for _n, _w in [("c", 16), ("bada", L * 48), ("n1g", L * 8), ("n2g", L * 8), ("nfg", 8), ("nag", L * 4),
               ("nbg", L * 4), ("ncg", L * 4), ("lbl", L * 4), ("bal", L * 4), ("bmask", 8)]:
    SMO[_n] = _o
    _o += _w
NSM = _o


class Eng:
    def __init__(self, name, h, sem):
        self.key = name
        self.h = h
        self.sem = sem
        self.cnt = 0
        self.seen = {}


class DSem:
    def __init__(self, name, sem):
        self.key = name
        self.sem = sem
        self.cnt = 0


class Buf:
    __slots__ = ("w", "r", "name")

    def __init__(self, name=""):
        self.w = {}
        self.r = {}
        self.name = name


class Builder:
    def __init__(self, layers, final, dbg=None):
        self.layers = layers
        self.final = final
        self.dbg = dbg
        self.nc = bass.Bass("TRN2", target_bir_lowering=False)
        self.es = ExitStack()
        nc = self.nc
        self.PE = Eng("pe", nc.tensor, self.sem("pe"))
        self.ACT = Eng("act", nc.scalar, self.sem("act"))
        self.DVE = Eng("dve", nc.vector, self.sem("dve"))
        self.SP = Eng("sp", nc.sync, None)
        self.POOL = Eng("pool", nc.gpsimd, None)
        self.dsems = []
        self.bank_i = 0
        self.dbg_on = False
        self.dbg_off = 0
        self.dbg_map = {}

    def sem(self, name):
        return self.es.enter_context(self.nc.semaphore(name))

    def dsem(self, name):
        d = DSem(name, self.sem(name))
        self.dsems.append(d)
        return d

    def sb(self, es, name, shape, dt):
        self.uid = getattr(self, "uid", 0) + 1
        return es.enter_context(self.nc.sbuf_tensor(f"{name}_{self.uid}", shape, dt))

    def _sync(self, eng, reads, writes):
        need = {}
        for b in reads:
            for k, ev in b.w.items():
                if k not in need or need[k][1] < ev[1]:
                    need[k] = ev
        for b in writes:
            for dct in (b.w, b.r):
                for k, ev in dct.items():
                    if k not in need or need[k][1] < ev[1]:
                        need[k] = ev
        for k, ev in need.items():
            sem, v = ev[0], ev[1]
            if k == eng.key and (eng is self.PE or eng.cnt - v >= 3 or ev[2] >= 64):
                continue
            if eng.seen.get(k, 0) >= v:
                continue
            eng.h.wait_ge(sem, v)
            eng.seen[k] = v

    def op(self, eng, fn, reads=(), writes=(), nfree=4096):
        self._sync(eng, reads, writes)
        ins = fn()
        eng.cnt += 1
        ins.then_inc(eng.sem, 1)
        ev = (eng.sem, eng.cnt, nfree)
        for b in reads:
            b.r[eng.key] = ev
        for b in writes:
            b.w[eng.key] = ev
            b.r = {}

    def dma_multi(self, q, items, ds):
        for (_, _, reads, writes) in items:
            self._sync(q, reads, writes)
        for (o, i, _, _) in items:
            q.h.dma_start(out=o, in_=i).then_inc(ds.sem, 16)
            ds.cnt += 16
        ev = (ds.sem, ds.cnt, 4096)
        for (_, _, reads, writes) in items:
            for b in reads:
                b.r[ds.key] = ev
            for b in writes:
                b.w[ds.key] = ev
                b.r = {}

    def dma(self, q, out, in_, ds, reads=(), writes=()):
        self.dma_multi(q, [(out, in_, reads, writes)], ds)

    def barrier(self):
        engs = [self.PE, self.ACT, self.DVE]
        for e in engs + [self.SP]:
            for o in engs:
                if o is e or o.cnt == 0:
                    continue
                if e.seen.get(o.key, 0) >= o.cnt:
                    continue
                e.h.wait_ge(o.sem, o.cnt)
                e.seen[o.key] = o.cnt
            for d in self.dsems:
                if d.cnt == 0 or e.seen.get(d.key, 0) >= d.cnt:
                    continue
                e.h.wait_ge(d.sem, d.cnt)
                e.seen[d.key] = d.cnt

    def mm(self, out, lhsT, rhs, start, stop, reads, writes):
        nc = self.nc
        self.op(self.PE, lambda: nc.tensor.matmul(out, lhsT=lhsT, rhs=rhs, start=start, stop=stop), reads, writes)

    def act(self, out, in_, func, reads, writes, bias=None, scale=None):
        nc = self.nc
        kw = {}
        if bias is not None:
            kw["bias"] = bias
        if scale is not None:
            kw["scale"] = scale
        self.op(self.ACT, lambda: nc.scalar.activation(out=out, in_=in_, func=func, **kw), reads, writes, nfree=out.free_size())

    def tt(self, out, in0, in1, opx, reads, writes):
        nc = self.nc
        self.op(self.DVE, lambda: nc.vector.tensor_tensor(out=out, in0=in0, in1=in1, op=opx), reads, writes, nfree=out.free_size())

    def ts(self, out, in0, s1, s2, op0, op1, reads, writes):
        nc = self.nc
        if op1 is None:
            self.op(self.DVE, lambda: nc.vector.tensor_scalar(out=out, in0=in0, scalar1=s1, scalar2=None, op0=op0), reads, writes, nfree=out.free_size())
        else:
            self.op(self.DVE, lambda: nc.vector.tensor_scalar(out=out, in0=in0, scalar1=s1, scalar2=s2, op0=op0, op1=op1), reads, writes, nfree=out.free_size())

    def stt(self, out, in0, scalar, in1, op0, op1, reads, writes):
        nc = self.nc
        self.op(self.DVE, lambda: nc.vector.scalar_tensor_tensor(out=out, in0=in0, scalar=scalar, in1=in1, op0=op0, op1=op1), reads, writes, nfree=out.free_size())

    def vcopy(self, out, in_, reads, writes):
        nc = self.nc
        self.op(self.DVE, lambda: nc.vector.tensor_copy(out=out, in_=in_), reads, writes, nfree=out.free_size())

    def memset(self, ap, val, writes):
        nc = self.nc
        self.op(self.DVE, lambda: nc.vector.memset(ap, val), (), writes, nfree=ap.free_size())

    def dump(self, name, ap, reads, ncols):
        if not self.dbg or not self.dbg_on:
            return
        o = self.dbg_off
        self.dbg_off += ncols
        assert self.dbg_off <= self.dbg
        self.dbg_map[name] = (o, ncols)
        self.dma(self.POOL, self.d_dbg[:, o:o + ncols], ap, self.ds_dbg, reads, ())

    def next_bank(self):
        i = self.bank_i
        self.bank_i = (i + 1) % len(self.banks)
        return self.banks[i], self.bank_b[i]

    def load_w(self, src_ap, nk=8):
        i = self.w_i
        self.w_i = (i + 1) % len(self.wslots)
        t, b, ds = self.wslots[i], self.wslot_b[i], self.wslot_ds[i]
        self.dma(self.POOL, t[:, 0:nk, :], src_ap, ds, (), (b,))
        return t, b

    def build(self):
        nc = self.nc
        es = self.es
        layers = self.layers
        NL = len(layers)
        dr = lambda name, shape, dt=F32, kind="ExternalInput": nc.dram_tensor(name, shape, dt, kind=kind).ap()
        self.d_xT = dr("xT", [2, 128, KC, T])
        self.d_sm = dr("smalls", [128, NSM])
        self.d_wal = dr("wal", [16, L, 2, 256])
        self.d_cmat = dr("cmat", [128, 6, 128])
        self.d_rot = dr("rot", [2, 128, T])
        self.d_cumB = dr("cumB", [2, 2, 128, T + 1])
        self.d_retf = dr("retf", [2, 2, 128, 3 * 128 + 16])
        self.d_wada = dr("wada", [NL, 16, 128, KC, 384])
        self.d_wz = dr("wz", [NL, NZG, 128, KC, 128])
        self.d_wp = dr("wp", [NL, 3, 8, 128, 4, 128])
        self.d_wo = dr("wo", [NL, 8, 128, KC, 128])
        self.d_wf1 = dr("wf1", [NL, NFC, 2, 128, KC, 128])
        self.d_wf2 = dr("wf2", [NL, 2, 8, 128, 11, 128])
        self.d_out = dr("yT", [2, 128, KC, T], F32, "ExternalOutput")
        self.d_yscr = dr("yscr", [12, 128, T], BF16, "Internal")
        if self.dbg:
            self.d_dbg = dr("dbg", [128, self.dbg], F32, "ExternalOutput")

        P = lambda name, shape, dt: self.sb(es, name, shape, dt)
        self.xT = P("xT_sb", [128, KC, T], F32)
        self.xTb = [[Buf() for _ in range(NB)] for _ in range(KC)]
        self.hT = P("hT_sb", [128, KC, T], BF16)
        self.hTb = [[Buf() for _ in range(NB)] for _ in range(KC)]
        self.sm = P("sm_sb", [128, NSM], F32)
        self.smb = Buf()
        self.cm = P("cm_sb", [128, 6, 128], BF16)
        self.cmb = Buf()
        self.modT = P("modT", [128, NL, 48, 2], F32)
        self.modb = Buf()
        self.scp = P("scp", [128, NL, 2, 8, 2], F32)
        self.lbv = P("lbv", [128, L, 4], F32)
        self.omlv = P("omlv", [128, L, 4], F32)
        self.nbal = P("nbal", [128, L * 4], F32)
        self.epsc = P("epsc", [128, 1], F32)
        self.onec = P("onec", [128, 1], F32)
        NWS = 5
        self.wslots = [P(f"ws{i}", [128, KC, 128], BF16) for i in range(NWS)]
        self.wslot_b = [Buf() for _ in range(NWS)]
        self.wslot_ds = [self.dsem(f"wsd{i}") for i in range(NWS)]
        self.w_i = 0
        self.banks = [es.enter_context(nc.psum_tensor(f"pb{i}", [128, 512], F32)) for i in range(7)]
        self.bank_b = [Buf() for _ in range(7)]
        self.pst = es.enter_context(nc.psum_tensor("pst", [128, 1024], BF16))
        self.pst_b = [Buf() for _ in range(8)]
        self.pst_i = 0
        self.ds_const = self.dsem("const")
        self.ds_constp = self.dsem("constp")
        self.ds_w2 = [self.dsem("w2a"), self.dsem("w2b")]
        self.ds_xin = self.dsem("xin")
        self.ds_tab = self.dsem("tab")
        self.ds_cum = self.dsem("cum")
        self.ds_cum2 = self.dsem("walsem")
        self.ds_retf = [self.dsem("retf0"), self.dsem("retf1")]
        self.ds_dbg = self.dsem("dbgsem")
        self.ds_y = self.dsem("yst")
        self.ds_yld = self.dsem("yld")
        self.ds_out = [self.dsem("out0"), self.dsem("out1")]
        self.ds_wada = [self.dsem("wada0"), self.dsem("wada1")]
        self.yscr_b = [Buf() for _ in range(12)]

        self.prologue()
        for s in range(2):
            self.load_x(s)
            for li, l in enumerate(layers):
                self.dbg_on = (s == 0 and li == 0)
                self.layer(s, li, l)
                self.dbg_on = False
            self.finish(s)
        for d in self.ds_out:
            if d.cnt:
                self.SP.h.wait_ge(d.sem, d.cnt)
        self.barrier()
        self.es.close()
        return nc

    def smv(self, name, w):
        o = SMO[name]
        return self.sm[:, o:o + w]

    def prologue(self):
        nc = self.nc
        self.dma_multi(self.SP, [(self.sm[:, :], self.d_sm[:, :], (), (self.smb,))], self.ds_const)
        self.dma(self.POOL, self.cm[:, :, :], self.d_cmat[:, :, :], self.ds_constp, (), (self.cmb,))
        self.ident = self.cm[:, 0, :]
        self.onesb = self.cm[:, 1, :]
        self.masks = {(1, 0): self.cm[:, 2, :], (1, 1): self.cm[:, 3, :], (8, 0): self.cm[:, 4, :], (8, 1): self.cm[:, 5, :]}
        smb = self.smb
        self.memset(self.epsc[:, :], EPS, (smb,))
        self.memset(self.onec[:, :], 1.0, (smb,))
        with ExitStack() as ps:
            cact = self.sb(ps, "cact", [128, 16], F32)
            ctmp = self.sb(ps, "ctmp", [128, 64], F32)
            wsl = [self.sb(ps, f"wada{i}", [128, KC, 384], F32) for i in range(2)]
            wslb = [Buf(), Buf()]
            cb = Buf()
            self.act(ctmp[:, 0:16], self.smv("c", 16), AF.Sigmoid, (smb,), (cb,))
            self.tt(cact[:, :], ctmp[:, 0:16], self.smv("c", 16), ALU.mult, (cb, smb), (cb,))
            e = ctmp[:, 16:32]
            self.act(e, self.smv("lbl", 16), AF.Exp, (smb,), (cb,))
            ssum = ctmp[:, 32:36]
            self.tt(ssum, e[:, 0:4], e[:, 4:8], ALU.add, (cb,), (cb,))
            self.tt(ssum, ssum, e[:, 8:12], ALU.add, (cb,), (cb,))
            self.tt(ssum, ssum, e[:, 12:16], ALU.add, (cb,), (cb,))
            rs = ctmp[:, 36:40]
            self.op(self.DVE, lambda: nc.vector.reciprocal(out=rs, in_=ssum), (cb,), (cb,), nfree=4)
            pr = ctmp[:, 40:56]
            for l in range(L):
                self.tt(pr[:, l * 4:(l + 1) * 4], e[:, l * 4:(l + 1) * 4], rs, ALU.mult, (cb,), (cb,))
            self.memset(self.lbv[:, 0, :], 0.0, (cb,))
            self.vcopy(self.lbv[:, 1, :], pr[:, 4:8], (cb,), (cb,))
            self.tt(self.lbv[:, 2, :], self.lbv[:, 1, :], pr[:, 8:12], ALU.add, (cb,), (cb,))
            self.tt(self.lbv[:, 3, :], self.lbv[:, 2, :], pr[:, 12:16], ALU.add, (cb,), (cb,))
            self.ts(self.omlv[:, :, :], self.lbv[:, :, :], -1.0, 1.0, ALU.mult, ALU.add, (cb,), (cb,))
            self.ts(self.nbal[:, :], self.smv("bal", L * 4), -1.0, None, ALU.mult, None, (smb,), (cb,))
            self.cb = cb
            for li, l in enumerate(self.layers):
                bank, bb = self.next_bank()
                for g in range(16):
                    i = g % 2
                    self.dma(self.SP, wsl[i][:, :, :], self.d_wada[li, g, :, :, :], self.ds_wada[i], (), (wslb[i],))
                    for mi in range(3):
                        m = g * 3 + mi
                        for kc in range(KC):
                            self.mm(bank[:, m * 2:(m + 1) * 2], wsl[i][:, kc, mi * 128:(mi + 1) * 128],
                                    cact[:, kc * 2:(kc + 1) * 2], kc == 0, kc == KC - 1, (wslb[i], cb), (bb,))
                o = SMO["bada"] + l * 48
                self.tt(self.modT[:, li, :, :], bank[:, 0:96].rearrange("p (m b) -> p m b", b=2),
                        self.sm[:, o:o + 48].unsqueeze(2).to_broadcast([128, 48, 2]), ALU.add, (bb, smb), (self.modb,))
                for sub in range(2):
                    sc = self.modT[:, li, 8 + 24 * sub:16 + 24 * sub, :]
                    og = SMO["n1g" if sub == 0 else "n2g"] + l * 8
                    self.stt(self.scp[:, li, sub, :, :], sc, 1.0, self.sm[:, og:og + 8].unsqueeze(2).to_broadcast([128, 8, 2]),
                             ALU.add, ALU.mult, (self.modb, smb), (self.modb,))
            self.barrier()

    def load_x(self, s):
        items = []
        for kc in range(KC):
            items.append((self.xT[:, kc, :], self.d_xT[s, :, kc, :], (), tuple(self.xTb[kc])))
        self.dma_multi(self.SP, items, self.ds_xin)

    def emit_norm(self, pes, scale_fn, shift_fn, sbufs):
        sq, sqb, rs_l, rsb_l, tf, tfb = sbufs
        for tb in range(NB):
            cs = slice(tb * 512, (tb + 1) * 512)
            bank, bb = self.next_bank()
            for kc in range(KC):
                i = kc % 2
                self.act(sq[i][:, :], self.xT[:, kc, cs], AF.Square, (self.xTb[kc][tb],), (sqb[i],))
                self.mm(bank[:, :], self.onesb, sq[i][:, :], kc == 0, kc == KC - 1, (sqb[i], self.cmb), (bb,))
            rs, rsb = rs_l[tb % 2], rsb_l[tb % 2]
            self.act(rs[:, :], bank[:, :], AF.Ln, (bb,), (rsb,), bias=self.epsc[:, 0:1], scale=1.0 / D)
            self.act(rs[:, :], rs[:, :], AF.Exp, (rsb,), (rsb,), scale=-0.5)
            if scale_fn is None:
                yield tb, rs, rsb
                continue
            for kc in range(KC):
                i = kc % 2
                self.stt(tf[i][:, :], self.xT[:, kc, cs], scale_fn(kc), rs[:, :], ALU.mult, ALU.mult,
                         (self.xTb[kc][tb], rsb, self.modb), (tfb[i],))
                if kc % 2 == 0:
                    self.act(self.hT[:, kc, cs], tf[i][:, :], AF.Identity, (tfb[i], self.modb), (self.hTb[kc][tb],),
                             bias=shift_fn(kc), scale=1.0)
                else:
                    self.ts(self.hT[:, kc, cs], tf[i][:, :], shift_fn(kc), None, ALU.add, None, (tfb[i], self.modb), (self.hTb[kc][tb],))

    def norm_to_h(self, es, li, sub, b):
        sq = [self.sb(es, f"nsq{sub}{i}", [128, 512], BF16) for i in range(2)]
        rs = [self.sb(es, f"nrs{sub}{i}", [128, 512], F32) for i in range(2)]
        tf = [self.sb(es, f"ntf{sub}{i}", [128, 512], F32) for i in range(2)]
        bufs = (sq, [Buf(), Buf()], rs, [Buf(), Buf()], tf, [Buf(), Buf()])
        scale_fn = lambda kc: self.scp[:, li, sub, kc, b:b + 1]
        shift_fn = lambda kc: self.modT[:, li, 24 * sub + kc, b:b + 1]
        for _ in self.emit_norm(es, scale_fn, shift_fn, bufs):
            pass

    def zgroup(self, li, name, evac):
        wt, wb = self.load_w(self.d_wz[li, ZIDX[name], :, :, :])
        for tb in range(NB):
            cs = slice(tb * 512, (tb + 1) * 512)
            bank, bb = self.next_bank()
            for kc in range(KC):
                self.mm(bank[:, :], wt[:, kc, :], self.hT[:, kc, cs], kc == 0, kc == KC - 1, (wb, self.hTb[kc][tb]), (bb,))
            evac(tb, cs, bank, bb)

    def vgroup(self, li, name, V, Vb, col):
        wt, wb = self.load_w(self.d_wz[li, ZIDX[name], :, :, :])
        for t4 in range(4):
            bank, bb = self.next_bank()
            for j in range(4):
                t = t4 * 4 + j
                for kc in range(KC):
                    self.mm(bank[:, j * 128:(j + 1) * 128], self.hT[:, kc, t * 128:(t + 1) * 128], wt[:, kc, :],
                            kc == 0, kc == KC - 1, (wb, self.hTb[kc][t // 4]), (bb,))
            self.act(V[:, t4 * 4:(t4 + 1) * 4, col:col + 128], bank[:, :].rearrange("p (j n) -> p j n", n=128), AF.Copy,
                     (bb,), (Vb,))

    def mixer_phase(self, s, li, l):
        nc = self.nc
        with ExitStack() as es:
            W = lambda name, shape, dt: self.sb(es, name, shape, dt)
            with ExitStack() as nes:
                self.norm_to_h(nes, li, 0, s)
                self.barrier()
            self.dump("hT0", self.hT[:, 0, :], tuple(self.hTb[0]), T)
            self.dump("lbv", self.lbv[:, :, :].rearrange("p l h -> p (l h)"), (self.cb,), 16)
            self.dump("omlv", self.omlv[:, :, :].rearrange("p l h -> p (l h)"), (self.cb,), 16)
            self.dump("lbl", self.smv("lbl", 16), (self.smb,), 16)
            m = {}
            m["q"] = W("m_q", [128, T], BF16)
            m["k"] = W("m_k", [128, T], BF16)
            m["CUM"] = W("m_cum", [128, T + 1], F32)
            m["tmp"] = W("m_tmp", [128, 4, 1024], F32)
            m["QH"] = W("m_QH", [128, T], BF16)
            m["KT"] = W("m_KT", [128, T], BF16)
            m["KH"] = W("m_KH", [128, T], BF16)
            m["V"] = W("m_V", [128, NT, 256], BF16)
            m["S"] = [W(f"m_S{i}", [128, 32 * 128], BF16) for i in range(2)]
            m["Sm"] = [W(f"m_Sm{i}", [128, 256], F32) for i in range(2)]
            m["KHt"] = [W(f"m_KHt{i}", [128, 1024], BF16) for i in range(2)]
            m["P"] = [W(f"m_P{i}", [128, 512], BF16) for i in range(2)]
            m["dec"] = W("m_dec", [128, 128], F32)
            m["dd"] = W("m_dd", [128, 128], F32)
            m["oT"] = W("m_oT", [128, 2, T], F32)
            m["wal"] = W("m_wal", [16, 512], BF16)
            m["retf"] = [W("m_retf", [128, 3 * 128 + 16], F32)] * 2
            m["gt"] = [m["tmp"][:, 0, i * 512:(i + 1) * 512] for i in range(2)]
            m["rs"] = m["tmp"][:, 1, 0:512]
            m["t1"] = m["tmp"][:, 1, 512:1024]
            sqv = m["tmp"][:, 2, :].bitcast(BF16)
            m["sq"] = [sqv[:, i * 512:(i + 1) * 512] for i in range(2)]
            m["yst"] = m["tmp"][:, 3, :].bitcast(BF16)
            B = {k: Buf(k) for k in ["q", "k", "CUM", "QH", "KT", "KH", "V", "dec", "dd", "wal"]}
            B["S"] = [Buf(), Buf()]
            B["retf"] = [Buf()] * 2
            B["tmp"] = [Buf() for _ in range(4)]
            B["rs"] = B["t1"] = B["tmp"][1]
            B["yst"] = B["tmp"][3]
            B["Sm"] = [Buf(), Buf()]
            B["KHt"] = [Buf(), Buf()]
            B["dS"] = [Buf(), Buf()]
            self.kh_i = 0
            self.p_i = 0
            B["P"] = [Buf(), Buf()]
            B["oT"] = [[Buf() for _ in range(NB)] for _ in range(2)]
            B["gt"] = [B["tmp"][0], B["tmp"][0]]
            B["sq"] = [B["tmp"][2], B["tmp"][2]]
            self.m, self.B = m, B
            self.dma(self.POOL, m["wal"][:, :], self.d_wal[:, l, :, :].rearrange("r n k -> r (n k)"), self.ds_cum2, (), (B["wal"],))
            self.memset(m["CUM"][:, 0:1], 0.0, (B["CUM"],))
            for h in range(4):
                self.mixer_A(s, li, l, h)
            for P in range(2):
                self.mixer_B(s, li, l, P)
            for P in range(2):
                self.mixer_C(s, li, l, P)
            self.barrier()

    def factors(self, d, nsub, cscale):
        m, B = self.m, self.B
        C = 128 // nsub
        nch = T // C
        CUM = m["CUM"]
        hc = nch // 2
        for hf in range(2):
            c0 = hf * hc
            tcol = slice(hf * 1024, (hf + 1) * 1024)
            Xv = CUM[:, 1 + hf * 1024:1 + (hf + 1) * 1024].rearrange("p (c k) -> p c k", k=C)
            Yv = CUM[:, hf * 1024:(hf + 1) * 1024].rearrange("p (c k) -> p c k", k=C)
            b0 = Yv[:, :, 0:1].to_broadcast([128, hc, C])
            b1 = Xv[:, :, C - 1:C].to_broadcast([128, hc, C])
            tD = m["tmp"][:, 0 + hf * 2, :]
            tE = m["tmp"][:, 1 + hf * 2, :]
            bD, bE = B["tmp"][0 + hf * 2], B["tmp"][1 + hf * 2]
            tDv = tD.rearrange("p (c k) -> p c k", k=C)
            if d == 0:
                self.tt(tDv, Xv, b0, ALU.subtract, (B["CUM"],), (bD,))
            else:
                self.tt(tDv, b1, Yv, ALU.subtract, (B["CUM"],), (bD,))
            self.act(tE, tD, AF.Exp, (bD,), (bE,), scale=cscale)
            self.tt(m["QH"][:, tcol], m["q"][:, tcol], tE, ALU.mult, (B["q"], bE), (B["QH"],))
            self.act(tE, tD, AF.Exp, (bD,), (bE,), scale=-cscale)
            self.tt(m["KT"][:, tcol], m["k"][:, tcol], tE, ALU.mult, (B["k"], bE), (B["KT"],))
            if d == 0:
                self.tt(tDv, b1, Xv, ALU.subtract, (B["CUM"],), (bD,))
            else:
                self.tt(tDv, Yv, b0, ALU.subtract, (B["CUM"],), (bD,))
            self.act(tE, tD, AF.Exp, (bD,), (bE,), scale=cscale)
            self.tt(m["KH"][:, tcol], m["k"][:, tcol], tE, ALU.mult, (B["k"], bE), (B["KH"],))
        e1 = CUM[:, 1:T + 1].rearrange("p (c k) -> p c k", k=C)[:, :, C - 1:C]
        e0 = CUM[:, 0:T].rearrange("p (c k) -> p c k", k=C)[:, :, 0:1]
        self.tt(m["dd"][:, 0:nch].unsqueeze(2), e1, e0, ALU.subtract, (B["CUM"],), (B["dd"],))
        self.act(m["dec"][:, 0:nch], m["dd"][:, 0:nch], AF.Exp, (B["dd"],), (B["dec"],), scale=cscale)

    def chain(self, d, nsub, Wd, qt, first):
        nc = self.nc
        m = self.m
        B = dict(self.B)
        B["S"] = self.B["S"][qt % 2]
        C = 128 // nsub
        nch = T // C
        qch = nch // 4
        S = m["S"][qt % 2]
        Sv = lambda c: S[:, (c - qt * qch) * Wd:(c - qt * qch + 1) * Wd]
        tiles = list(range(qt * 4, qt * 4 + 4))
        if d == 1:
            tiles = tiles[::-1]
        c_first = tiles[0] * nsub if d == 0 else tiles[0] * nsub + nsub - 1
        if first:
            self.cur = 0
            self.memset(m["Sm"][0][:, 0:Wd], 0.0, (B["Sm"][0],))
            self.memset(Sv(c_first), 0.0, (B["S"],))
        else:
            self.act(Sv(c_first), m["Sm"][self.cur][:, 0:Wd], AF.Copy, (B["Sm"][self.cur],), (B["S"],))
        bmask = self.smv("bmask", 8)

        def prep_tile(t):
            ts_ = slice(t * 128, (t + 1) * 128)
            pi = self.pst_i
            self.pst_i = (pi + 1) % 8
            pslot = self.pst[:, pi * 128:(pi + 1) * 128]
            pb = self.pst_b[pi]
            self.op(self.PE, lambda: nc.tensor.transpose(pslot, self.m["KH"][:, ts_], self.ident), (B["KH"], self.cmb), (pb,))
            ki = self.kh_i
            self.kh_i = 1 - ki
            kht, khb = m["KHt"][ki], B["KHt"][ki]
            if nsub == 1:
                self.vcopy(kht[:, 0:128], pslot, (pb,), (khb,))
            else:
                self.tt(kht[:, 0:nsub * 128].rearrange("p (a n) -> p a n", n=128), pslot.unsqueeze(1).to_broadcast([128, nsub, 128]),
                        bmask.unsqueeze(2).to_broadcast([128, nsub, 128]), ALU.mult, (pb, self.smb), (khb,))
            return kht, khb

        nxt_prep = prep_tile(tiles[0])
        for it, t in enumerate(tiles):
            kht, khb = nxt_prep
            if it + 1 < len(tiles):
                nxt_prep = prep_tile(tiles[it + 1])
            if nsub == 1:
                bank, bb = self.next_bank()
                self.mm(bank[:, 0:Wd], kht[:, 0:128], m["V"][:, t, 0:Wd], True, True, (khb, B["V"]), (bb,))
                dsl = [(bank[:, 0:Wd], bb)]
            else:
                dsl = []
                for g in range(nsub // 4):
                    bank, bb = self.next_bank()
                    for a4 in range(4):
                        a = g * 4 + a4
                        self.mm(bank[:, a4 * 128:(a4 + 1) * 128], kht[:, a * 128:(a + 1) * 128], m["V"][:, t, 0:128], True, True,
                                (khb, B["V"]), (bb,))
                        dsl.append((bank[:, a4 * 128:(a4 + 1) * 128], bb))
            subs = range(nsub) if d == 0 else range(nsub - 1, -1, -1)
            for a in subs:
                c = t * nsub + a
                nxt = c + 1 if d == 0 else c - 1
                if nxt < 0 or nxt >= nch:
                    continue
                cur = self.cur
                nx = 1 - cur
                dsa, dsb = dsl[a]
                inq = qt * qch <= nxt < (qt + 1) * qch
                if nsub > 1:
                    if inq:
                        self.stt(Sv(nxt), Sv(c), m["dec"][:, c:c + 1], dsa, ALU.mult, ALU.add, (B["S"], B["dec"], dsb), (B["S"],))
                    else:
                        self.stt(m["Sm"][nx][:, 0:Wd], Sv(c), m["dec"][:, c:c + 1], dsa, ALU.mult, ALU.add,
                                 (B["S"], B["dec"], dsb), (B["Sm"][nx],))
                        self.cur = nx
                    continue
                self.stt(m["Sm"][nx][:, 0:Wd], m["Sm"][cur][:, 0:Wd], m["dec"][:, c:c + 1], dsa,
                         ALU.mult, ALU.add, (B["Sm"][cur], B["dec"], dsb), (B["Sm"][nx],))
                if inq:
                    self.act(Sv(nxt), m["Sm"][nx][:, 0:Wd], AF.Copy, (B["Sm"][nx],), (B["S"],))
                self.cur = nx

    def outphase(self, d, nsub, Wd, hh, rows, vcol, qt):
        m = self.m
        B = dict(self.B)
        B["S"] = self.B["S"][qt % 2]
        C = 128 // nsub
        qch = (T // C) // 4
        mask = self.masks[(nsub, d)]
        S = m["S"][qt % 2]
        t4 = qt
        bankP, bPb = self.next_bank()
        for j in range(4):
            t = t4 * 4 + j
            ts_ = slice(t * 128, (t + 1) * 128)
            self.mm(bankP[:, j * 128:(j + 1) * 128], m["KT"][rows, ts_], m["QH"][rows, ts_], True, True,
                    (B["KT"], B["QH"]), (bPb,))
        pi = self.p_i
        self.p_i = 1 - pi
        Pt, Ptb = m["P"][pi], B["P"][pi]
        self.tt(Pt[:, :].rearrange("p (j n) -> p j n", n=128), bankP[:, :].rearrange("p (j n) -> p j n", n=128),
                mask.unsqueeze(1).to_broadcast([128, 4, 128]), ALU.mult, (bPb, self.cmb), (Ptb,))
        bankO, bOb = self.next_bank()
        for j in range(4):
            t = t4 * 4 + j
            self.mm(bankO[:, j * 128:(j + 1) * 128], m["V"][:, t, vcol:vcol + 128], Pt[:, j * 128:(j + 1) * 128], True, False,
                    (B["V"], Ptb), (bOb,))
            for a in range(nsub):
                c = t * nsub + a
                sl = c - qt * qch
                self.mm(bankO[:, j * 128 + a * C:j * 128 + (a + 1) * C],
                        S[rows, sl * Wd + hh * 128:sl * Wd + (hh + 1) * 128], m["QH"][rows, c * C:(c + 1) * C],
                        False, True, (B["S"], B["QH"]), (bOb,))
        osl = m["oT"][:, hh, t4 * 512:(t4 + 1) * 512]
        ob = B["oT"][hh][t4]
        if d == 0:
            self.act(osl, bankO[:, :], AF.Copy, (bOb,), (ob,))
        else:
            self.tt(osl, bankO[:, :], osl, ALU.add, (bOb, ob), (ob,))

    def scan_dir(self, d, nsub, Wd, heads):
        quarters = (0, 1, 2, 3) if d == 0 else (3, 2, 1, 0)
        self.chain(d, nsub, Wd, quarters[0], True)
        for iq, qt in enumerate(quarters):
            if iq + 1 < 4:
                self.chain(d, nsub, Wd, quarters[iq + 1], False)
            for (hh, rows, vcol) in heads:
                self.outphase(d, nsub, Wd, hh, rows, vcol, qt)

    @staticmethod
    def _merge_into(dst, srcs):
        for sb_ in srcs:
            for dct in (sb_.w, sb_.r):
                for k, ev in dct.items():
                    if k not in dst.w or dst.w[k][1] < ev[1]:
                        dst.w[k] = ev

    def yphase(self, s, li, hh, gate_name, gate_func, gain_ap, groupnorm, yidx):
        m, B = self.m, self.B
        tmp = m["tmp"]
        gt = [tmp[:, 0, 0:512], tmp[:, 0, 512:1024]]
        rs = [tmp[:, 1, 0:512], tmp[:, 1, 512:1024]]
        t1 = tmp[:, 2, 0:512]
        sqv = tmp[:, 2, 512:1024].bitcast(BF16)
        sq = [sqv[:, 0:512], sqv[:, 512:1024]]
        yst = m["yst"]
        f = {k: Buf(k) for k in ["gt0", "gt1", "rs0", "rs1", "t1", "sq0", "sq1"]}
        alias = {"gt0": 0, "gt1": 0, "rs0": 1, "rs1": 1, "t1": 2, "sq0": 2, "sq1": 2}
        for k, q in alias.items():
            self._merge_into(f[k], [B["tmp"][q]])
        gtb = [f["gt0"], f["gt1"]]
        rsb = [f["rs0"], f["rs1"]]
        sqb = [f["sq0"], f["sq1"]]
        t1b = f["t1"]
        wt, wb = self.load_w(self.d_wz[li, ZIDX[gate_name], :, :, :])
        for pr in range(2):
            blocks = [pr * 2, pr * 2 + 1]
            for i, tb in enumerate(blocks):
                cs = slice(tb * 512, (tb + 1) * 512)
                bank, bb = self.next_bank()
                for kc in range(KC):
                    self.mm(bank[:, :], wt[:, kc, :], self.hT[:, kc, cs], kc == 0, kc == KC - 1, (wb, self.hTb[kc][tb]), (bb,))
                self.act(gt[i], bank[:, :], gate_func, (bb,), (gtb[i],))
            banksN = []
            for i, tb in enumerate(blocks):
                cs = slice(tb * 512, (tb + 1) * 512)
                o = m["oT"][:, hh, cs]
                ob = B["oT"][hh][tb]
                if groupnorm:
                    self.vcopy(sq[i], o, (ob,), (sqb[i],))
                    bankM, bMb = self.next_bank()
                    self.mm(bankM[:, :], self.onesb, sq[i], True, True, (sqb[i], self.cmb), (bMb,))
                    self.stt(o, bankM[:, :], -1.0 / 128.0, o, ALU.mult, ALU.add, (bMb, ob), (ob,))
                self.tt(sq[i], o, o, ALU.mult, (ob,), (sqb[i],))
                bankN, bNb = self.next_bank()
                self.mm(bankN[:, :], self.onesb, sq[i], True, True, (sqb[i], self.cmb), (bNb,))
                banksN.append((bankN, bNb))
            for i, tb in enumerate(blocks):
                bankN, bNb = banksN[i]
                self.act(rs[i], bankN[:, :], AF.Ln, (bNb, self.smb), (rsb[i],), bias=self.epsc[:, 0:1], scale=1.0 / 128.0)
                self.act(rs[i], rs[i], AF.Exp, (rsb[i],), (rsb[i],), scale=-0.5)
            for i, tb in enumerate(blocks):
                cs = slice(tb * 512, (tb + 1) * 512)
                o = m["oT"][:, hh, cs]
                ob = B["oT"][hh][tb]
                self.stt(t1, o, gain_ap, rs[i], ALU.mult, ALU.mult, (ob, rsb[i], self.smb), (t1b,))
                self.tt(yst[:, cs], t1, gt[i], ALU.mult, (t1b, gtb[i]), (B["yst"],))
        self.dma(self.SP, self.d_yscr[yidx, :, :], yst[:, :], self.ds_y, (B["yst"],), (self.yscr_b[yidx],))
        for k, q in alias.items():
            self._merge_into(B["tmp"][q], [f[k]])

    def mixer_A(self, s, li, l, h):
        nc = self.nc
        m, B = self.m, self.B
        self.zgroup(li, f"QA{h}", lambda tb, cs, bank, bb: self.act(m["q"][:, cs], bank[:, :], AF.Silu, (bb,), (B["q"],)))
        self.vgroup(li, f"VA{h}", m["V"], B["V"], 0)
        if h == 0:
            self.dump("A_q", m["q"][:, :], (B["q"],), T)
        fs = m["oT"][:, 1, :]
        fb = B["oT"][1]
        lbc = self.lbv[:, l, h:h + 1]
        omc = self.omlv[:, l, h:h + 1]
        for d in range(2):
            self.zgroup(li, f"FF{h}" if d == 0 else f"FB{h}",
                        lambda tb, cs, bank, bb: self.act(fs[:, cs], bank[:, :], AF.Sigmoid, (bb,), (fb[tb],)))
            self.ts(fs, fs, omc, lbc, ALU.mult, ALU.add, tuple(fb) + (self.cb,), tuple(fb))
            self.ts(m["k"][:, :], fs, -1.0, 1.0, ALU.mult, ALU.add, tuple(fb), (B["k"],))
            self.act(fs, fs, AF.Ln, tuple(fb), tuple(fb))
            self.op(self.DVE, lambda: nc.vector.tensor_tensor_scan(out=m["CUM"][:, 1:T + 1], data0=self.onec[:, 0:1].to_broadcast([128, T]),
                                                                    data1=fs, initial=0.0, op0=ALU.mult, op1=ALU.add),
                    tuple(fb) + (self.smb,), (B["CUM"],))
            if h == 0:
                self.dump(f"A_k{d}", m["k"][:, :], (B["k"],), T)
                self.dump(f"A_cum{d}", m["CUM"][:, 1:T + 1], (B["CUM"],), T)
            self.factors(d, 8, 1.0)
            if h == 0:
                self.dump(f"A_QH{d}", m["QH"][:, :], (B["QH"],), T)
                self.dump(f"A_KT{d}", m["KT"][:, :], (B["KT"],), T)
                self.dump(f"A_KH{d}", m["KH"][:, :], (B["KH"],), T)
            self.scan_dir(d, 8, 128, [(0, slice(0, 128), 0)])
            if h == 0:
                self.dump(f"A_o{d}", m["oT"][:, 0, :], tuple(B["oT"][0]), T)
        o = SMO["nag"] + l * 4 + h
        self.yphase(s, li, 0, f"GA{h}", AF.Sigmoid, self.sm[:, o:o + 1], False, 0 + h)

    def rotary(self, li, dst, dstb, gname, gsname, scl):
        m, B = self.m, self.B
        ZA = m["tmp"][:, 0:2, :].rearrange("p a n -> p (a n)")
        ZB = m["tmp"][:, 2:4, :].rearrange("p a n -> p (a n)")
        bA = (B["tmp"][0], B["tmp"][1])
        bB = (B["tmp"][2], B["tmp"][3])
        self.zgroup(li, gname, lambda tb, cs, bank, bb: self.act(ZA[:, cs], bank[:, :], AF.Copy, (bb,), (bA[tb // 2],), scale=scl))
        self.zgroup(li, gsname, lambda tb, cs, bank, bb: self.act(ZB[:, cs], bank[:, :], AF.Copy, (bb,), (bB[tb // 2],), scale=scl))
        cosb = tuple(B["oT"][0])
        sinb = tuple(B["oT"][1])
        self.tt(ZA, ZA, m["oT"][:, 0, :], ALU.mult, bA + cosb, bA)
        self.tt(ZB, ZB, m["oT"][:, 1, :], ALU.mult, bB + sinb, bB)
        self.tt(dst[:, :], ZA, ZB, ALU.add, bA + bB, (dstb,))

    def mixer_B(self, s, li, l, P):
        m, B = self.m, self.B
        self.dma_multi(self.SP, [(m["oT"][:, 0, :], self.d_rot[0, :, :], (), tuple(B["oT"][0])),
                                 (m["oT"][:, 1, :], self.d_rot[1, :, :], (), tuple(B["oT"][1]))], self.ds_tab)
        self.rotary(li, m["q"], B["q"], f"QB{P}", f"QBs{P}", 1.0)
        self.rotary(li, m["k"], B["k"], f"KB{P}", f"KBs{P}", 0.125)
        self.vgroup(li, f"VB{2 * P}", m["V"], B["V"], 0)
        self.vgroup(li, f"VB{2 * P + 1}", m["V"], B["V"], 128)
        self.dma(self.SP, m["retf"][0][:, :], self.d_retf[P, 0, :, :], self.ds_retf[0], (), (B["retf"][0],))
        for d in range(2):
            tb_, tbb = m["retf"][d], B["retf"][d]
            v3 = lambda ap: ap.rearrange("p (c k) -> p c k", k=128)
            bc = lambda j: tb_[:, j * 128:(j + 1) * 128].unsqueeze(1).to_broadcast([128, NT, 128])
            self.tt(v3(m["QH"][:, :]), v3(m["q"][:, :]), bc(0), ALU.mult, (B["q"], tbb), (B["QH"],))
            self.tt(v3(m["KT"][:, :]), v3(m["k"][:, :]), bc(1), ALU.mult, (B["k"], tbb), (B["KT"],))
            self.tt(v3(m["KH"][:, :]), v3(m["k"][:, :]), bc(2), ALU.mult, (B["k"], tbb), (B["KH"],))
            self.vcopy(m["dec"][:, 0:NT], tb_[:, 384:384 + NT], (tbb,), (B["dec"],))
            if d == 0:
                self.dma(self.SP, m["retf"][1][:, :], self.d_retf[P, 1, :, :], self.ds_retf[0], (), (B["retf"][1],))
            self.scan_dir(d, 1, 256, [(hh, slice(hh * 64, hh * 64 + 64), hh * 128) for hh in range(2)])
        for hh in range(2):
            h = 2 * P + hh
            o = SMO["nbg"] + l * 4 + h
            self.yphase(s, li, hh, f"GB{h}", AF.Silu, self.sm[:, o:o + 1], True, 4 + h)

    def mixer_C(self, s, li, l, P):
        nc = self.nc
        m, B = self.m, self.B
        self.zgroup(li, f"QC{P}", lambda tb, cs, bank, bb: self.act(m["q"][:, cs], bank[:, :], AF.Copy, (bb,), (B["q"],), scale=0.125))
        self.zgroup(li, f"KC{P}", lambda tb, cs, bank, bb: self.act(m["k"][:, cs], bank[:, :], AF.Copy, (bb,), (B["k"],)))
        self.vgroup(li, f"VC{2 * P}", m["V"], B["V"], 0)
        self.vgroup(li, f"VC{2 * P + 1}", m["V"], B["V"], 128)
        X = m["tmp"][:, 0:2, :].rearrange("p a n -> p (a n)")
        bX = (B["tmp"][0], B["tmp"][1])
        for d in range(2):
            wt, wb = self.load_w(self.d_wz[li, ZIDX["LR"], :, :, :])
            for tb in range(NB):
                cs = slice(tb * 512, (tb + 1) * 512)
                bank, bb = self.next_bank()
                for kc in range(KC):
                    self.mm(bank[0:16, :], wt[:, kc, d * 16:(d + 1) * 16], self.hT[:, kc, cs], kc == 0, kc == KC - 1,
                            (wb, self.hTb[kc][tb]), (bb,))
                self.act(m["KH"][0:16, cs], bank[0:16, :], AF.Copy, (bb,), (B["KH"],))
            wo = d * 256 + P * 128
            nb_ = self.nbal[:, l * 4 + d * 2 + P:l * 4 + d * 2 + P + 1]
            for tb in range(NB):
                cs = slice(tb * 512, (tb + 1) * 512)
                bank, bb = self.next_bank()
                self.mm(bank[:, :], m["wal"][0:16, wo:wo + 128], m["KH"][0:16, cs], True, True, (B["wal"], B["KH"]), (bb,))
                self.act(X[:, cs], bank[:, :], AF.Exp, (bb, self.cb), (bX[tb // 2],), bias=nb_, scale=-1.0)
            self.act(X, X, AF.Ln, bX + (self.smb,), bX, bias=self.onec[:, 0:1], scale=1.0)
            self.op(self.DVE, lambda: nc.vector.tensor_tensor_scan(out=m["CUM"][:, 1:T + 1], data0=self.onec[:, 0:1].to_broadcast([128, T]),
                                                                    data1=X, initial=0.0, op0=ALU.mult, op1=ALU.add),
                    bX + (self.smb,), (B["CUM"],))
            self.factors(d, 1, -1.0 / 16.0)
            self.scan_dir(d, 1, 256, [(hh, slice(hh * 64, hh * 64 + 64), hh * 128) for hh in range(2)])
        for hh in range(2):
            h = 2 * P + hh
            o = SMO["ncg"] + l * 4 + h
            self.yphase(s, li, hh, f"GC{h}", AF.Silu, self.sm[:, o:o + 1], False, 8 + h)

    def merge_phase(self, s, li, l):
        with ExitStack() as es:
            W = lambda name, shape, dt: self.sb(es, name, shape, dt)
            yall = W("g_yall", [128, 12, T], BF16)
            yb = Buf()
            mg = W("g_mg", [128, KC, T], BF16)
            mgb = [Buf() for _ in range(NB)]
            gs = [W(f"g_gs{i}", [128, 512], F32) for i in range(2)]
            gsb = [Buf() for _ in range(2)]
            acc = W("g_acc", [128, T], F32)
            accb = [Buf() for _ in range(NB)]
            items = [(yall[:, i, :], self.d_yscr[i, :, :], (self.yscr_b[i],), (yb,)) for i in range(12)]
            self.dma_multi(self.SP, items, self.ds_yld)
            for i in range(12):
                self.dump(f"y{i}", yall[:, i, :], (yb,), T)
            mnames = ["GMa", "GMb", "GMc"]
            for oc in range(8):
                for mi in range(3):
                    wg = self.load_w(self.d_wz[li, ZIDX[f"{mnames[mi]}{oc}"], :, :, :])
                    wp = self.load_w(self.d_wp[li, mi, oc, :, :, :], nk=4)
                    for tb in range(NB):
                        cs = slice(tb * 512, (tb + 1) * 512)
                        gi = tb % 2
                        bankG, bGb = self.next_bank()
                        for kc in range(KC):
                            self.mm(bankG[:, :], wg[0][:, kc, :], self.hT[:, kc, cs], kc == 0, kc == KC - 1,
                                    (wg[1], self.hTb[kc][tb]), (bGb,))
                        self.act(gs[gi][:, :], bankG[:, :], AF.Sigmoid, (bGb,), (gsb[gi],))
                        bankP, bPb = self.next_bank()
                        for kc in range(4):
                            self.mm(bankP[:, :], wp[0][:, kc, :], yall[:, mi * 4 + kc, cs], kc == 0, kc == 3,
                                    (wp[1], yb), (bPb,))
                        if mi == 0:
                            self.tt(acc[:, cs], bankP[:, :], gs[gi][:, :], ALU.mult, (bPb, gsb[gi]), (accb[tb],))
                        elif mi == 1:
                            self.tt(gs[gi][:, :], bankP[:, :], gs[gi][:, :], ALU.mult, (bPb, gsb[gi]), (gsb[gi],))
                            self.tt(acc[:, cs], acc[:, cs], gs[gi][:, :], ALU.add, (accb[tb], gsb[gi]), (accb[tb],))
                        else:
                            self.tt(gs[gi][:, :], bankP[:, :], gs[gi][:, :], ALU.mult, (bPb, gsb[gi]), (gsb[gi],))
                            self.tt(mg[:, oc, cs], acc[:, cs], gs[gi][:, :], ALU.add, (accb[tb], gsb[gi]), (mgb[tb],))
            self.dump("mg0", mg[:, 0, :], tuple(mgb), T)
            for oc in range(8):
                wt, wb = self.load_w(self.d_wo[li, oc, :, :, :])
                g1 = self.modT[:, li, 16 + oc, s:s + 1]
                for tb in range(NB):
                    cs = slice(tb * 512, (tb + 1) * 512)
                    bank, bb = self.next_bank()
                    for kc in range(KC):
                        self.mm(bank[:, :], wt[:, kc, :], mg[:, kc, cs], kc == 0, kc == KC - 1, (wb, mgb[tb]), (bb,))
                    xb = self.xTb[oc][tb]
                    self.stt(self.xT[:, oc, cs], bank[:, :], g1, self.xT[:, oc, cs], ALU.mult, ALU.add, (bb, xb, self.modb), (xb,))
            self.barrier()

    def ffn_phase(self, s, li, l):
        with ExitStack() as es:
            W = lambda name, shape, dt: self.sb(es, name, shape, dt)
            with ExitStack() as nes:
                self.norm_to_h(nes, li, 1, s)
                self.barrier()
            actT = W("f_act", [128, 11, T], BF16)
            ab = [Buf() for _ in range(NB)]
            st = [W(f"f_st{i}", [128, 512], F32) for i in range(2)]
            stb = [Buf(), Buf()]
            w2 = [W(f"f_w2{i}", [128, 11, 128], BF16) for i in range(2)]
            w2b = [Buf(), Buf()]
            w2ds = self.ds_w2
            for gi in range(2):
                for fcl in range(11):
                    fc = gi * 11 + fcl
                    wg, wgb = self.load_w(self.d_wf1[li, fc, 0, :, :, :])
                    wu, wub = self.load_w(self.d_wf1[li, fc, 1, :, :, :])
                    for tb in range(NB):
                        cs = slice(tb * 512, (tb + 1) * 512)
                        i = tb % 2
                        bankG, bGb = self.next_bank()
                        for kc in range(KC):
                            self.mm(bankG[:, :], wg[:, kc, :], self.hT[:, kc, cs], kc == 0, kc == KC - 1, (wgb, self.hTb[kc][tb]), (bGb,))
                        bankU, bUb = self.next_bank()
                        for kc in range(KC):
                            self.mm(bankU[:, :], wu[:, kc, :], self.hT[:, kc, cs], kc == 0, kc == KC - 1, (wub, self.hTb[kc][tb]), (bUb,))
                        self.act(st[i][:, :], bankG[:, :], AF.Silu, (bGb,), (stb[i],))
                        self.tt(actT[:, fcl, cs], bankU[:, :], st[i][:, :], ALU.mult, (bUb, stb[i]), (ab[tb],))
                for oc in range(8):
                    i = oc % 2
                    self.dma(self.POOL, w2[i][:, :, :], self.d_wf2[li, gi, oc, :, :, :], w2ds[i], (), (w2b[i],))
                    g2 = self.modT[:, li, 40 + oc, s:s + 1]
                    for tb in range(NB):
                        cs = slice(tb * 512, (tb + 1) * 512)
                        bank, bb = self.next_bank()
                        for fcl in range(11):
                            self.mm(bank[:, :], w2[i][:, fcl, :], actT[:, fcl, cs], fcl == 0, fcl == 10, (w2b[i], ab[tb]), (bb,))
                        xb = self.xTb[oc][tb]
                        self.stt(self.xT[:, oc, cs], bank[:, :], g2, self.xT[:, oc, cs], ALU.mult, ALU.add, (bb, xb, self.modb), (xb,))
            self.barrier()

    def layer(self, s, li, l):
        self.mixer_phase(s, li, l)
        self.merge_phase(s, li, l)
        self.ffn_phase(s, li, l)

    def finish(self, s):
        with ExitStack() as es:
            W = lambda name, shape, dt: self.sb(es, name, shape, dt)
            ost = [W(f"o_st{i}", [128, 512], F32) for i in range(2)]
            ostb = [Buf(), Buf()]
            k = 0
            if self.final:
                sq = [W(f"osq{i}", [128, 512], BF16) for i in range(2)]
                rs = [W(f"ors{i}", [128, 512], F32) for i in range(2)]
                bufs = (sq, [Buf(), Buf()], rs, [Buf(), Buf()], None, None)
                o = SMO["nfg"]
                for tb, rs_, rsb in self.emit_norm(es, None, None, bufs):
                    cs = slice(tb * 512, (tb + 1) * 512)
                    for kc in range(KC):
                        i = k % 2
                        k += 1
                        self.stt(ost[i][:, :], self.xT[:, kc, cs], self.sm[:, o + kc:o + kc + 1], rs_[:, :], ALU.mult, ALU.mult,
                                 (self.xTb[kc][tb], rsb, self.smb), (ostb[i],))
                        self.dma(self.SP, self.d_out[s, :, kc, cs], ost[i][:, :], self.ds_out[i], (ostb[i],), ())
            else:
                for kc in range(KC):
                    self.dma(self.SP, self.d_out[s, :, kc, :], self.xT[:, kc, :], self.ds_out[kc % 2], tuple(self.xTb[kc]), ())
            self.barrier()


def _tile_w(w, nk):
    n = w.shape[1]
    return np.ascontiguousarray(w.reshape(nk, 128, n).transpose(1, 0, 2))


def _const_tables():
    j = np.arange(128)[:, None]
    i = np.arange(128)[None, :]
    same = (j // 16) == (i // 16)
    cm = np.zeros((128, 6, 128), np.float32)
    cm[:, 0, :] = np.eye(128)
    cm[:, 1, :] = 1.0
    cm[:, 2, :] = (j <= i)
    cm[:, 3, :] = (j > i)
    cm[:, 4, :] = (j <= i) & same
    cm[:, 5, :] = (j > i) & same
    pos = np.arange(T, dtype=np.float32)
    inv_freq = (np.float32(10000.0) ** (-np.arange(0, 64, 2, dtype=np.float32) / np.float32(64))).astype(np.float32)
    ang = (pos[:, None] * inv_freq[None, :]).astype(np.float32)
    cos = np.cos(ang).astype(np.float32).T
    sin = np.sin(ang).astype(np.float32).T
    rot = np.zeros((2, 128, T), np.float32)
    for p in range(128):
        d = p % 64
        rot[0, p] = cos[d % 32]
        rot[1, p] = -sin[d % 32] if d < 32 else sin[d % 32]
    cumB = np.zeros((2, 2, 128, T + 1), np.float32)
    tt = np.arange(T + 1, dtype=np.float64)
    for P in range(2):
        for dd in range(2):
            for p in range(128):
                h = 2 * P + p // 64
                hx = h if dd == 0 else 3 - h
                lg = np.log1p(-np.exp2(-5.0 - hx))
                cumB[P, dd, p] = (tt * lg).astype(np.float32)
    retf = np.zeros((2, 2, 128, 3 * 128 + 16), np.float32)
    kk = np.arange(128, dtype=np.float64)
    for P in range(2):
        for dd in range(2):
            for p in range(128):
                h = 2 * P + p // 64
                hx = h if dd == 0 else 3 - h
                lg = np.log1p(-np.exp2(-5.0 - hx))
                if dd == 0:
                    eq, ekh = np.exp((kk + 1) * lg), np.exp((127 - kk) * lg)
                    ekt = np.exp(-(kk + 1) * lg)
                else:
                    eq, ekh = np.exp((128 - kk) * lg), np.exp(kk * lg)
                    ekt = np.exp(-(128 - kk) * lg)
                retf[P, dd, p, 0:128] = eq
                retf[P, dd, p, 128:256] = ekt
                retf[P, dd, p, 256:384] = ekh
                retf[P, dd, p, 384:400] = np.exp(128 * lg)
    return cm, rot, cumB, retf


def _prep_layer_weights(inp, l):
    w_in = inp["w_in"][l]
    wz = np.stack([_tile_w(w_in[:, ZG[n]], 8) for n in ZNAMES], 0)
    wps = [inp["w_pa"][l], inp["w_pb"][l], inp["w_pc"][l]]
    wp = np.stack([np.stack([_tile_w(w[:, oc * 128:(oc + 1) * 128], 4) for oc in range(8)], 0) for w in wps], 0)
    wo = np.stack([_tile_w(inp["w_out"][l][:, oc * 128:(oc + 1) * 128], 8) for oc in range(8)], 0)
    wfi = inp["w_ffn_in"][l]
    wf1 = np.stack([np.stack([_tile_w(wfi[:, fc * 128:(fc + 1) * 128], 8),
                              _tile_w(wfi[:, DFF + fc * 128:DFF + (fc + 1) * 128], 8)], 0) for fc in range(NFC)], 0)
    wfo = inp["w_ffn_out"][l]
    wf2 = np.stack([np.stack([_tile_w(wfo[gi * 1408:(gi + 1) * 1408, oc * 128:(oc + 1) * 128], 11) for oc in range(8)], 0)
                    for gi in range(2)], 0)
    wada = np.stack([_tile_w(inp["w_ada"][l][:, g * 384:(g + 1) * 384], 8) for g in range(16)], 0)
    return dict(wz=wz, wp=wp, wo=wo, wf1=wf1, wf2=wf2, wada=wada)


def _smalls(inp, c2):
    sm = np.zeros((128, NSM), np.float32)
    col = lambda v, n: np.ascontiguousarray(v.reshape(n, 128).T)
    sm[:, SMO["c"]:SMO["c"] + 16] = np.stack([col(c2[b], 8) for b in range(2)], 2).reshape(128, 16)
    for l in range(L):
        sm[:, SMO["bada"] + l * 48:SMO["bada"] + (l + 1) * 48] = col(inp["b_ada"][l], 48)
        sm[:, SMO["n1g"] + l * 8:SMO["n1g"] + (l + 1) * 8] = col(inp["norm1_g"][l], 8)
        sm[:, SMO["n2g"] + l * 8:SMO["n2g"] + (l + 1) * 8] = col(inp["norm2_g"][l], 8)
        sm[:, SMO["nag"] + l * 4:SMO["nag"] + (l + 1) * 4] = col(inp["norm_a_g"][l], 4)
        sm[:, SMO["nbg"] + l * 4:SMO["nbg"] + (l + 1) * 4] = col(inp["norm_b_g"][l], 4)
        sm[:, SMO["ncg"] + l * 4:SMO["ncg"] + (l + 1) * 4] = col(inp["norm_c_g"][l], 4)
        sm[:, SMO["lbl"] + l * 4:SMO["lbl"] + (l + 1) * 4] = col(inp["lb_logits"][l], 4)
        for d in range(2):
            sm[:, SMO["bal"] + l * 4 + d * 2:SMO["bal"] + l * 4 + d * 2 + 2] = col(inp["b_alpha"][l, d], 2)
    sm[:, SMO["nfg"]:SMO["nfg"] + 8] = col(inp["norm_f_g"], 8)
    for a in range(8):
        sm[a * 16:(a + 1) * 16, SMO["bmask"] + a] = 1.0
    return sm


_CACHE = {}


def _get_nc(layers, final):
    key = (tuple(layers), final)
    if key not in _CACHE:
        _CACHE[key] = Builder(list(layers), final).build()
    return _CACHE[key]


def _run(inp, xT_cores, layers, final):
    nc = _get_nc(layers, final)
    cm, rot, cumB, retf = _const_tables()
    lw = [_prep_layer_weights(inp, l) for l in layers]
    shared = {k: np.stack([w[k] for w in lw], 0) for k in lw[0]}
    wal = np.ascontiguousarray(inp["w_alpha"].transpose(2, 0, 1, 3))
    in_maps = []
    for core in range(NCORES):
        c2 = inp["c"][2 * core:2 * core + 2]
        mp = dict(xT=xT_cores[core], smalls=_smalls(inp, c2), wal=wal, cmat=cm, rot=rot, cumB=cumB, retf=retf)
        mp.update(shared)
        in_maps.append(mp)
    res = run_bass_kernel_spmd(nc, in_maps, core_ids=list(range(NCORES)))
    return [np.asarray(r["yT"]) for r in res.results]


def _to_xT(x):
    out = []
    for core in range(NCORES):
        xs = x[2 * core:2 * core + 2]
        out.append(np.ascontiguousarray(xs.transpose(0, 2, 1).reshape(2, KC, 128, T).transpose(0, 2, 1, 3)))
    return out


def _from_xT(outs):
    res = np.empty((16, T, D), np.float32)
    for core in range(NCORES):
        o = outs[core]
        res[2 * core:2 * core + 2] = o.transpose(0, 2, 1, 3).reshape(2, D, T).transpose(0, 2, 1)
    return res


FUSED = True


def kernel(**inputs):
    inp = {k: np.asarray(v, dtype=np.float32) for k, v in inputs.items()}
    xT = _to_xT(inp["x"])
    if FUSED:
        outs = _run(inp, xT, [0, 1, 2, 3], True)
    else:
        outs = xT
        for l in range(L):
            outs = _run(inp, outs, [l], l == L - 1)
    return _from_xT(outs)
```

```python
import numpy as np
from contextlib import ExitStack
import concourse.bass as bass
import concourse.mybir as mybir
from concourse.bass_utils import run_bass_kernel_spmd

F32 = mybir.dt.float32
BF16 = mybir.dt.bfloat16
AF = mybir.ActivationFunctionType
ALU = mybir.AluOpType

D = 1024
T = 2048
L = 4
NT = 16
NB = 4
KC = 8
NCORES = 8
DFF = 2816
NFC = 22
EPS = 1e-6

OFF = dict(a_q=0, a_ff=512, a_fb=1024, a_i=1536, a_g=2048, b_q=2560, b_k=2816, b_v=3072, b_g=3584,
           c_q=4096, c_k=4352, c_v=4608, c_g=5120, c_lr=5632, gate_a=5664, gate_b=6688, gate_c=7712)


def _zgroup_cols():
    g = {}
    ar = np.arange(128)
    for h in range(4):
        g[f"QA{h}"] = OFF["a_q"] + h * 128 + ar
        g[f"FF{h}"] = OFF["a_ff"] + h * 128 + ar
        g[f"FB{h}"] = OFF["a_fb"] + h * 128 + ar
        g[f"VA{h}"] = OFF["a_i"] + h * 128 + ar
        g[f"GA{h}"] = OFF["a_g"] + h * 128 + ar
        g[f"VB{h}"] = OFF["b_v"] + h * 128 + ar
        g[f"GB{h}"] = OFF["b_g"] + h * 128 + ar
        g[f"VC{h}"] = OFF["c_v"] + h * 128 + ar
        g[f"GC{h}"] = OFF["c_g"] + h * 128 + ar
    sw = (ar // 64) * 64 + ((ar % 64) + 32) % 64
    for P in range(2):
        g[f"QB{P}"] = OFF["b_q"] + P * 128 + ar
        g[f"QBs{P}"] = OFF["b_q"] + P * 128 + sw
        g[f"KB{P}"] = OFF["b_k"] + P * 128 + ar
        g[f"KBs{P}"] = OFF["b_k"] + P * 128 + sw
        g[f"QC{P}"] = OFF["c_q"] + P * 128 + ar
        g[f"KC{P}"] = OFF["c_k"] + P * 128 + ar
    g["LR"] = OFF["c_lr"] + (ar % 32)
    for oc in range(8):
        g[f"GMa{oc}"] = OFF["gate_a"] + oc * 128 + ar
        g[f"GMb{oc}"] = OFF["gate_b"] + oc * 128 + ar
        g[f"GMc{oc}"] = OFF["gate_c"] + oc * 128 + ar
    return g


ZG = _zgroup_cols()
ZNAMES = list(ZG.keys())
ZIDX = {n: i for i, n in enumerate(ZNAMES)}
NZG = len(ZNAMES)

SMO = {}
_o = 0
for _n, _w in [("c", 16), ("bada", L * 48), ("n1g", L * 8), ("n2g", L * 8), ("nfg", 8), ("nag", L * 4),
               ("nbg", L * 4), ("ncg", L * 4), ("lbl", L * 4), ("bal", L * 4), ("bmask", 8)]:
    SMO[_n] = _o
    _o += _w
NSM = _o


class Eng:
    def __init__(self, name, h, sem):
        self.key = name
        self.h = h
        self.sem = sem
        self.cnt = 0
        self.seen = {}


class DSem:
    def __init__(self, name, sem):
        self.key = name
        self.sem = sem
        self.cnt = 0


class Buf:
    __slots__ = ("w", "r", "name")

    def __init__(self, name=""):
        self.w = {}
        self.r = {}
        self.name = name


class Builder:
    def __init__(self, layers, final, dbg=None):
        self.layers = layers
        self.final = final
        self.dbg = dbg
        self.nc = bass.Bass("TRN2", target_bir_lowering=False)
        self.es = ExitStack()
        nc = self.nc
        self.PE = Eng("pe", nc.tensor, self.sem("pe"))
        self.ACT = Eng("act", nc.scalar, self.sem("act"))
        self.DVE = Eng("dve", nc.vector, self.sem("dve"))
        self.SP = Eng("sp", nc.sync, None)
        self.POOL = Eng("pool", nc.gpsimd, None)
        self.dsems = []
        self.bank_i = 0
        self.dbg_on = False
        self.dbg_off = 0
        self.dbg_map = {}

    def sem(self, name):
        return self.es.enter_context(self.nc.semaphore(name))

    def dsem(self, name):
        d = DSem(name, self.sem(name))
        self.dsems.append(d)
        return d

    def sb(self, es, name, shape, dt):
        self.uid = getattr(self, "uid", 0) + 1
        return es.enter_context(self.nc.sbuf_tensor(f"{name}_{self.uid}", shape, dt))

    def _sync(self, eng, reads, writes):
        need = {}
        for b in reads:
            for k, ev in b.w.items():
                if k not in need or need[k][1] < ev[1]:
                    need[k] = ev
        for b in writes:
            for dct in (b.w, b.r):
                for k, ev in dct.items():
                    if k not in need or need[k][1] < ev[1]:
                        need[k] = ev
        for k, ev in need.items():
            sem, v = ev[0], ev[1]
            if k == eng.key and (eng is self.PE or eng.cnt - v >= 3 or ev[2] >= 64):
                continue
            if eng.seen.get(k, 0) >= v:
                continue
            eng.h.wait_ge(sem, v)
            eng.seen[k] = v

    def op(self, eng, fn, reads=(), writes=(), nfree=4096):
        self._sync(eng, reads, writes)
        ins = fn()
        eng.cnt += 1
        ins.then_inc(eng.sem, 1)
        ev = (eng.sem, eng.cnt, nfree)
        for b in reads:
            b.r[eng.key] = ev
        for b in writes:
            b.w[eng.key] = ev
            b.r = {}

    def dma_multi(self, q, items, ds):
        for (_, _, reads, writes) in items:
            self._sync(q, reads, writes)
        for (o, i, _, _) in items:
            q.h.dma_start(out=o, in_=i).then_inc(ds.sem, 16)
            ds.cnt += 16
        ev = (ds.sem, ds.cnt, 4096)
        for (_, _, reads, writes) in items:
            for b in reads:
                b.r[ds.key] = ev
            for b in writes:
                b.w[ds.key] = ev
                b.r = {}

    def dma(self, q, out, in_, ds, reads=(), writes=()):
        self.dma_multi(q, [(out, in_, reads, writes)], ds)

    def barrier(self):
        engs = [self.PE, self.ACT, self.DVE]
        for e in engs + [self.SP]:
            for o in engs:
                if o is e or o.cnt == 0:
                    continue
                if e.seen.get(o.key, 0) >= o.cnt:
                    continue
                e.h.wait_ge(o.sem, o.cnt)
                e.seen[o.key] = o.cnt
            for d in self.dsems:
                if d.cnt == 0 or e.seen.get(d.key, 0) >= d.cnt:
                    continue
                e.h.wait_ge(d.sem, d.cnt)
                e.seen[d.key] = d.cnt

    def mm(self, out, lhsT, rhs, start, stop, reads, writes):
        nc = self.nc
        self.op(self.PE, lambda: nc.tensor.matmul(out, lhsT=lhsT, rhs=rhs, start=start, stop=stop), reads, writes)

    def act(self, out, in_, func, reads, writes, bias=None, scale=None):
        nc = self.nc
        kw = {}
        if bias is not None:
            kw["bias"] = bias
        if scale is not None:
            kw["scale"] = scale
        self.op(self.ACT, lambda: nc.scalar.activation(out=out, in_=in_, func=func, **kw), reads, writes, nfree=out.free_size())

    def tt(self, out, in0, in1, opx, reads, writes):
        nc = self.nc
        self.op(self.DVE, lambda: nc.vector.tensor_tensor(out=out, in0=in0, in1=in1, op=opx), reads, writes, nfree=out.free_size())

    def ts(self, out, in0, s1, s2, op0, op1, reads, writes):
        nc = self.nc
        if op1 is None:
            self.op(self.DVE, lambda: nc.vector.tensor_scalar(out=out, in0=in0, scalar1=s1, scalar2=None, op0=op0), reads, writes, nfree=out.free_size())
        else:
            self.op(self.DVE, lambda: nc.vector.tensor_scalar(out=out, in0=in0, scalar1=s1, scalar2=s2, op0=op0, op1=op1), reads, writes, nfree=out.free_size())

    def stt(self, out, in0, scalar, in1, op0, op1, reads, writes):
        nc = self.nc
        self.op(self.DVE, lambda: nc.vector.scalar_tensor_tensor(out=out, in0=in0, scalar=scalar, in1=in1, op0=op0, op1=op1), reads, writes, nfree=out.free_size())

    def vcopy(self, out, in_, reads, writes):
        nc = self.nc
        self.op(self.DVE, lambda: nc.vector.tensor_copy(out=out, in_=in_), reads, writes, nfree=out.free_size())

    def memset(self, ap, val, writes):
        nc = self.nc
        self.op(self.DVE, lambda: nc.vector.memset(ap, val), (), writes, nfree=ap.free_size())

    def dump(self, name, ap, reads, ncols):
        if not self.dbg or not self.dbg_on:
            return
        o = self.dbg_off
        self.dbg_off += ncols
        assert self.dbg_off <= self.dbg
        self.dbg_map[name] = (o, ncols)
        self.dma(self.POOL, self.d_dbg[:, o:o + ncols], ap, self.ds_dbg, reads, ())

    def next_bank(self):
        i = self.bank_i
        self.bank_i = (i + 1) % len(self.banks)
        return self.banks[i], self.bank_b[i]

    def load_w(self, src_ap, nk=8):
        i = self.w_i
        self.w_i = (i + 1) % len(self.wslots)
        t, b, ds = self.wslots[i], self.wslot_b[i], self.wslot_ds[i]
        self.dma(self.POOL, t[:, 0:nk, :], src_ap, ds, (), (b,))
        return t, b

    def build(self):
        nc = self.nc
        es = self.es
        layers = self.layers
        NL = len(layers)
        dr = lambda name, shape, dt=F32, kind="ExternalInput": nc.dram_tensor(name, shape, dt, kind=kind).ap()
        self.d_xT = dr("xT", [2, 128, KC, T])
        self.d_sm = dr("smalls", [128, NSM])
        self.d_wal = dr("wal", [16, L, 2, 256])
        self.d_cmat = dr("cmat", [128, 6, 128])
        self.d_rot = dr("rot", [2, 128, T])
        self.d_cumB = dr("cumB", [2, 2, 128, T + 1])
        self.d_retf = dr("retf", [2, 2, 128, 3 * 128 + 16])
        self.d_wada = dr("wada", [NL, 16, 128, KC, 384])
        self.d_wz = dr("wz", [NL, NZG, 128, KC, 128])
        self.d_wp = dr("wp", [NL, 3, 8, 128, 4, 128])
        self.d_wo = dr("wo", [NL, 8, 128, KC, 128])
        self.d_wf1 = dr("wf1", [NL, NFC, 2, 128, KC, 128])
        self.d_wf2 = dr("wf2", [NL, 2, 8, 128, 11, 128])
        self.d_out = dr("yT", [2, 128, KC, T], F32, "ExternalOutput")
        self.d_yscr = dr("yscr", [12, 128, T], BF16, "Internal")
        if self.dbg:
            self.d_dbg = dr("dbg", [128, self.dbg], F32, "ExternalOutput")

        P = lambda name, shape, dt: self.sb(es, name, shape, dt)
        self.xT = P("xT_sb", [128, KC, T], F32)
        self.xTb = [[Buf() for _ in range(NB)] for _ in range(KC)]
        self.hT = P("hT_sb", [128, KC, T], BF16)
        self.hTb = [[Buf() for _ in range(NB)] for _ in range(KC)]
        self.sm = P("sm_sb", [128, NSM], F32)
        self.smb = Buf()
        self.cm = P("cm_sb", [128, 6, 128], BF16)
        self.cmb = Buf()
        self.modT = P("modT", [128, NL, 48, 2], F32)
        self.modb = Buf()
        self.scp = P("scp", [128, NL, 2, 8, 2], F32)
        self.lbv = P("lbv", [128, L, 4], F32)
        self.omlv = P("omlv", [128, L, 4], F32)
        self.nbal = P("nbal", [128, L * 4], F32)
        self.epsc = P("epsc", [128, 1], F32)
        self.onec = P("onec", [128, 1], F32)
        NWS = 5
        self.wslots = [P(f"ws{i}", [128, KC, 128], BF16) for i in range(NWS)]
        self.wslot_b = [Buf() for _ in range(NWS)]
        self.wslot_ds = [self.dsem(f"wsd{i}") for i in range(NWS)]
        self.w_i = 0
        self.banks = [es.enter_context(nc.psum_tensor(f"pb{i}", [128, 512], F32)) for i in range(7)]
        self.bank_b = [Buf() for _ in range(7)]
        self.pst = es.enter_context(nc.psum_tensor("pst", [128, 1024], BF16))
        self.pst_b = [Buf() for _ in range(8)]
        self.pst_i = 0
        self.ds_const = self.dsem("const")
        self.ds_constp = self.dsem("constp")
        self.ds_w2 = [self.dsem("w2a"), self.dsem("w2b")]
        self.ds_xin = self.dsem("xin")
        self.ds_tab = self.dsem("tab")
        self.ds_cum = self.dsem("cum")
        self.ds_cum2 = self.dsem("walsem")
        self.ds_retf = [self.dsem("retf0"), self.dsem("retf1")]
        self.ds_dbg = self.dsem("dbgsem")
        self.ds_y = self.dsem("yst")
        self.ds_yld = self.dsem("yld")
        self.ds_out = [self.dsem("out0"), self.dsem("out1")]
        self.ds_wada = [self.dsem("wada0"), self.dsem("wada1")]
        self.yscr_b = [Buf() for _ in range(12)]

        self.prologue()
        for s in range(2):
            self.load_x(s)
            for li, l in enumerate(layers):
                self.dbg_on = (s == 0 and li == 0)
                self.layer(s, li, l)
                self.dbg_on = False
            self.finish(s)
        for d in self.ds_out:
            if d.cnt:
                self.SP.h.wait_ge(d.sem, d.cnt)
        self.barrier()
        self.es.close()
        return nc

    def smv(self, name, w):
        o = SMO[name]
        return self.sm[:, o:o + w]

    def prologue(self):
        nc = self.nc
        self.dma_multi(self.SP, [(self.sm[:, :], self.d_sm[:, :], (), (self.smb,))], self.ds_const)
        self.dma(self.POOL, self.cm[:, :, :], self.d_cmat[:, :, :], self.ds_constp, (), (self.cmb,))
        self.ident = self.cm[:, 0, :]
        self.onesb = self.cm[:, 1, :]
        self.masks = {(1, 0): self.cm[:, 2, :], (1, 1): self.cm[:, 3, :], (8, 0): self.cm[:, 4, :], (8, 1): self.cm[:, 5, :]}
        smb = self.smb
        self.memset(self.epsc[:, :], EPS, (smb,))
        self.memset(self.onec[:, :], 1.0, (smb,))
        with ExitStack() as ps:
            cact = self.sb(ps, "cact", [128, 16], F32)
            ctmp = self.sb(ps, "ctmp", [128, 64], F32)
            wsl = [self.sb(ps, f"wada{i}", [128, KC, 384], F32) for i in range(2)]
            wslb = [Buf(), Buf()]
            cb = Buf()
            self.act(ctmp[:, 0:16], self.smv("c", 16), AF.Sigmoid, (smb,), (cb,))
            self.tt(cact[:, :], ctmp[:, 0:16], self.smv("c", 16), ALU.mult, (cb, smb), (cb,))
            e = ctmp[:, 16:32]
            self.act(e, self.smv("lbl", 16), AF.Exp, (smb,), (cb,))
            ssum = ctmp[:, 32:36]
            self.tt(ssum, e[:, 0:4], e[:, 4:8], ALU.add, (cb,), (cb,))
            self.tt(ssum, ssum, e[:, 8:12], ALU.add, (cb,), (cb,))
            self.tt(ssum, ssum, e[:, 12:16], ALU.add, (cb,), (cb,))
            rs = ctmp[:, 36:40]
            self.op(self.DVE, lambda: nc.vector.reciprocal(out=rs, in_=ssum), (cb,), (cb,), nfree=4)
            pr = ctmp[:, 40:56]
            for l in range(L):
                self.tt(pr[:, l * 4:(l + 1) * 4], e[:, l * 4:(l + 1) * 4], rs, ALU.mult, (cb,), (cb,))
            self.memset(self.lbv[:, 0, :], 0.0, (cb,))
            self.vcopy(self.lbv[:, 1, :], pr[:, 4:8], (cb,), (cb,))
            self.tt(self.lbv[:, 2, :], self.lbv[:, 1, :], pr[:, 8:12], ALU.add, (cb,), (cb,))
            self.tt(self.lbv[:, 3, :], self.lbv[:, 2, :], pr[:, 12:16], ALU.add, (cb,), (cb,))
            self.ts(self.omlv[:, :, :], self.lbv[:, :, :], -1.0, 1.0, ALU.mult, ALU.add, (cb,), (cb,))
            self.ts(self.nbal[:, :], self.smv("bal", L * 4), -1.0, None, ALU.mult, None, (smb,), (cb,))
            self.cb = cb
            for li, l in enumerate(self.layers):
                bank, bb = self.next_bank()
                for g in range(16):
                    i = g % 2
                    self.dma(self.SP, wsl[i][:, :, :], self.d_wada[li, g, :, :, :], self.ds_wada[i], (), (wslb[i],))
                    for mi in range(3):
                        m = g * 3 + mi
                        for kc in range(KC):
                            self.mm(bank[:, m * 2:(m + 1) * 2], wsl[i][:, kc, mi * 128:(mi + 1) * 128],
                                    cact[:, kc * 2:(kc + 1) * 2], kc == 0, kc == KC - 1, (wslb[i], cb), (bb,))
                o = SMO["bada"] + l * 48
                self.tt(self.modT[:, li, :, :], bank[:, 0:96].rearrange("p (m b) -> p m b", b=2),
                        self.sm[:, o:o + 48].unsqueeze(2).to_broadcast([128, 48, 2]), ALU.add, (bb, smb), (self.modb,))
                for sub in range(2):
                    sc = self.modT[:, li, 8 + 24 * sub:16 + 24 * sub, :]
                    og = SMO["n1g" if sub == 0 else "n2g"] + l * 8
                    self.stt(self.scp[:, li, sub, :, :], sc, 1.0, self.sm[:, og:og + 8].unsqueeze(2).to_broadcast([128, 8, 2]),
                             ALU.add, ALU.mult, (self.modb, smb), (self.modb,))
            self.barrier()

    def load_x(self, s):
        items = []
        for kc in range(KC):
            items.append((self.xT[:, kc, :], self.d_xT[s, :, kc, :], (), tuple(self.xTb[kc])))
        self.dma_multi(self.SP, items, self.ds_xin)

    def emit_norm(self, pes, scale_fn, shift_fn, sbufs):
        sq, sqb, rs_l, rsb_l, tf, tfb = sbufs
        for tb in range(NB):
            cs = slice(tb * 512, (tb + 1) * 512)
            bank, bb = self.next_bank()
            for kc in range(KC):
                i = kc % 2
                self.act(sq[i][:, :], self.xT[:, kc, cs], AF.Square, (self.xTb[kc][tb],), (sqb[i],))
                self.mm(bank[:, :], self.onesb, sq[i][:, :], kc == 0, kc == KC - 1, (sqb[i], self.cmb), (bb,))
            rs, rsb = rs_l[tb % 2], rsb_l[tb % 2]
            self.act(rs[:, :], bank[:, :], AF.Ln, (bb,), (rsb,), bias=self.epsc[:, 0:1], scale=1.0 / D)
            self.act(rs[:, :], rs[:, :], AF.Exp, (rsb,), (rsb,), scale=-0.5)
            if scale_fn is None:
                yield tb, rs, rsb
                continue
            for kc in range(KC):
                i = kc % 2
                self.stt(tf[i][:, :], self.xT[:, kc, cs], scale_fn(kc), rs[:, :], ALU.mult, ALU.mult,
                         (self.xTb[kc][tb], rsb, self.modb), (tfb[i],))
                if kc % 2 == 0:
                    self.act(self.hT[:, kc, cs], tf[i][:, :], AF.Identity, (tfb[i], self.modb), (self.hTb[kc][tb],),
                             bias=shift_fn(kc), scale=1.0)
                else:
                    self.ts(self.hT[:, kc, cs], tf[i][:, :], shift_fn(kc), None, ALU.add, None, (tfb[i], self.modb), (self.hTb[kc][tb],))

    def norm_to_h(self, es, li, sub, b):
        sq = [self.sb(es, f"nsq{sub}{i}", [128, 512], BF16) for i in range(2)]
        rs = [self.sb(es, f"nrs{sub}{i}", [128, 512], F32) for i in range(2)]
        tf = [self.sb(es, f"ntf{sub}{i}", [128, 512], F32) for i in range(2)]
        bufs = (sq, [Buf(), Buf()], rs, [Buf(), Buf()], tf, [Buf(), Buf()])
        scale_fn = lambda kc: self.scp[:, li, sub, kc, b:b + 1]
        shift_fn = lambda kc: self.modT[:, li, 24 * sub + kc, b:b + 1]
        for _ in self.emit_norm(es, scale_fn, shift_fn, bufs):
            pass

    def zgroup(self, li, name, evac):
        wt, wb = self.load_w(self.d_wz[li, ZIDX[name], :, :, :])
        for tb in range(NB):
            cs = slice(tb * 512, (tb + 1) * 512)
            bank, bb = self.next_bank()
            for kc in range(KC):
                self.mm(bank[:, :], wt[:, kc, :], self.hT[:, kc, cs], kc == 0, kc == KC - 1, (wb, self.hTb[kc][tb]), (bb,))
            evac(tb, cs, bank, bb)

    def vgroup(self, li, name, V, Vb, col):
        wt, wb = self.load_w(self.d_wz[li, ZIDX[name], :, :, :])
        for t4 in range(4):
            bank, bb = self.next_bank()
            for j in range(4):
                t = t4 * 4 + j
                for kc in range(KC):
                    self.mm(bank[:, j * 128:(j + 1) * 128], self.hT[:, kc, t * 128:(t + 1) * 128], wt[:, kc, :],
                            kc == 0, kc == KC - 1, (wb, self.hTb[kc][t // 4]), (bb,))
            self.act(V[:, t4 * 4:(t4 + 1) * 4, col:col + 128], bank[:, :].rearrange("p (j n) -> p j n", n=128), AF.Copy,
                     (bb,), (Vb,))

    def mixer_phase(self, s, li, l):
        nc = self.nc
        with ExitStack() as es:
            W = lambda name, shape, dt: self.sb(es, name, shape, dt)
            with ExitStack() as nes:
                self.norm_to_h(nes, li, 0, s)
                self.barrier()
            self.dump("hT0", self.hT[:, 0, :], tuple(self.hTb[0]), T)
            self.dump("lbv", self.lbv[:, :, :].rearrange("p l h -> p (l h)"), (self.cb,), 16)
            self.dump("omlv", self.omlv[:, :, :].rearrange("p l h -> p (l h)"), (self.cb,), 16)
            self.dump("lbl", self.smv("lbl", 16), (self.smb,), 16)
            m = {}
            m["q"] = W("m_q", [128, T], BF16)
            m["k"] = W("m_k", [128, T], BF16)
            m["CUM"] = W("m_cum", [128, T + 1], F32)
            m["tmp"] = W("m_tmp", [128, 4, 1024], F32)
            m["QH"] = W("m_QH", [128, T], BF16)
            m["KT"] = W("m_KT", [128, T], BF16)
            m["KH"] = W("m_KH", [128, T], BF16)
            m["V"] = W("m_V", [128, NT, 256], BF16)
            m["S"] = [W(f"m_S{i}", [128, 32 * 128], BF16) for i in range(2)]
            m["Sm"] = [W(f"m_Sm{i}", [128, 256], F32) for i in range(2)]
            m["KHt"] = [W(f"m_KHt{i}", [128, 1024], BF16) for i in range(2)]
            m["P"] = [W(f"m_P{i}", [128, 512], BF16) for i in range(2)]
            m["dec"] = W("m_dec", [128, 128], F32)
            m["dd"] = W("m_dd", [128, 128], F32)
            m["oT"] = W("m_oT", [128, 2, T], F32)
            m["wal"] = W("m_wal", [16, 512], BF16)
            m["retf"] = [W("m_retf", [128, 3 * 128 + 16], F32)] * 2
            m["gt"] = [m["tmp"][:, 0, i * 512:(i + 1) * 512] for i in range(2)]
            m["rs"] = m["tmp"][:, 1, 0:512]
            m["t1"] = m["tmp"][:, 1, 512:1024]
            sqv = m["tmp"][:, 2, :].bitcast(BF16)
            m["sq"] = [sqv[:, i * 512:(i + 1) * 512] for i in range(2)]
            m["yst"] = m["tmp"][:, 3, :].bitcast(BF16)
            B = {k: Buf(k) for k in ["q", "k", "CUM", "QH", "KT", "KH", "V", "dec", "dd", "wal"]}
            B["S"] = [Buf(), Buf()]
            B["retf"] = [Buf()] * 2
            B["tmp"] = [Buf() for _ in range(4)]
            B["rs"] = B["t1"] = B["tmp"][1]
            B["yst"] = B["tmp"][3]
            B["Sm"] = [Buf(), Buf()]
            B["KHt"] = [Buf(), Buf()]
            B["dS"] = [Buf(), Buf()]
            self.kh_i = 0
            self.p_i = 0
            B["P"] = [Buf(), Buf()]
            B["oT"] = [[Buf() for _ in range(NB)] for _ in range(2)]
            B["gt"] = [B["tmp"][0], B["tmp"][0]]
            B["sq"] = [B["tmp"][2], B["tmp"][2]]
            self.m, self.B = m, B
            self.dma(self.POOL, m["wal"][:, :], self.d_wal[:, l, :, :].rearrange("r n k -> r (n k)"), self.ds_cum2, (), (B["wal"],))
            self.memset(m["CUM"][:, 0:1], 0.0, (B["CUM"],))
            for h in range(4):
                self.mixer_A(s, li, l, h)
            for P in range(2):
                self.mixer_B(s, li, l, P)
            for P in range(2):
                self.mixer_C(s, li, l, P)
            self.barrier()

    def factors(self, d, nsub, cscale):
        m, B = self.m, self.B
        C = 128 // nsub
        nch = T // C
        CUM = m["CUM"]
        hc = nch // 2
        for hf in range(2):
            c0 = hf * hc
            tcol = slice(hf * 1024, (hf + 1) * 1024)
            Xv = CUM[:, 1 + hf * 1024:1 + (hf + 1) * 1024].rearrange("p (c k) -> p c k", k=C)
            Yv = CUM[:, hf * 1024:(hf + 1) * 1024].rearrange("p (c k) -> p c k", k=C)
            b0 = Yv[:, :, 0:1].to_broadcast([128, hc, C])
            b1 = Xv[:, :, C - 1:C].to_broadcast([128, hc, C])
            tD = m["tmp"][:, 0 + hf * 2, :]
            tE = m["tmp"][:, 1 + hf * 2, :]
            bD, bE = B["tmp"][0 + hf * 2], B["tmp"][1 + hf * 2]
            tDv = tD.rearrange("p (c k) -> p c k", k=C)
            if d == 0:
                self.tt(tDv, Xv, b0, ALU.subtract, (B["CUM"],), (bD,))
            else:
                self.tt(tDv, b1, Yv, ALU.subtract, (B["CUM"],), (bD,))
            self.act(tE, tD, AF.Exp, (bD,), (bE,), scale=cscale)
            self.tt(m["QH"][:, tcol], m["q"][:, tcol], tE, ALU.mult, (B["q"], bE), (B["QH"],))
            self.act(tE, tD, AF.Exp, (bD,), (bE,), scale=-cscale)
            self.tt(m["KT"][:, tcol], m["k"][:, tcol], tE, ALU.mult, (B["k"], bE), (B["KT"],))
            if d == 0:
                self.tt(tDv, b1, Xv, ALU.subtract, (B["CUM"],), (bD,))
            else:
                self.tt(tDv, Yv, b0, ALU.subtract, (B["CUM"],), (bD,))
            self.act(tE, tD, AF.Exp, (bD,), (bE,), scale=cscale)
            self.tt(m["KH"][:, tcol], m["k"][:, tcol], tE, ALU.mult, (B["k"], bE), (B["KH"],))
        e1 = CUM[:, 1:T + 1].rearrange("p (c k) -> p c k", k=C)[:, :, C - 1:C]
        e0 = CUM[:, 0:T].rearrange("p (c k) -> p c k", k=C)[:, :, 0:1]
        self.tt(m["dd"][:, 0:nch].unsqueeze(2), e1, e0, ALU.subtract, (B["CUM"],), (B["dd"],))
        self.act(m["dec"][:, 0:nch], m["dd"][:, 0:nch], AF.Exp, (B["dd"],), (B["dec"],), scale=cscale)

    def chain(self, d, nsub, Wd, qt, first):
        nc = self.nc
        m = self.m
        B = dict(self.B)
        B["S"] = self.B["S"][qt % 2]
        C = 128 // nsub
        nch = T // C
        qch = nch // 4
        S = m["S"][qt % 2]
        Sv = lambda c: S[:, (c - qt * qch) * Wd:(c - qt * qch + 1) * Wd]
        tiles = list(range(qt * 4, qt * 4 + 4))
        if d == 1:
            tiles = tiles[::-1]
        c_first = tiles[0] * nsub if d == 0 else tiles[0] * nsub + nsub - 1
        if first:
            self.cur = 0
            self.memset(m["Sm"][0][:, 0:Wd], 0.0, (B["Sm"][0],))
            self.memset(Sv(c_first), 0.0, (B["S"],))
        else:
            self.act(Sv(c_first), m["Sm"][self.cur][:, 0:Wd], AF.Copy, (B["Sm"][self.cur],), (B["S"],))
        bmask = self.smv("bmask", 8)

        def prep_tile(t):
            ts_ = slice(t * 128, (t + 1) * 128)
            pi = self.pst_i
            self.pst_i = (pi + 1) % 8
            pslot = self.pst[:, pi * 128:(pi + 1) * 128]
            pb = self.pst_b[pi]
            self.op(self.PE, lambda: nc.tensor.transpose(pslot, self.m["KH"][:, ts_], self.ident), (B["KH"], self.cmb), (pb,))
            ki = self.kh_i
            self.kh_i = 1 - ki
            kht, khb = m["KHt"][ki], B["KHt"][ki]
            if nsub == 1:
                self.vcopy(kht[:, 0:128], pslot, (pb,), (khb,))
            else:
                self.tt(kht[:, 0:nsub * 128].rearrange("p (a n) -> p a n", n=128), pslot.unsqueeze(1).to_broadcast([128, nsub, 128]),
                        bmask.unsqueeze(2).to_broadcast([128, nsub, 128]), ALU.mult, (pb, self.smb), (khb,))
            return kht, khb

        nxt_prep = prep_tile(tiles[0])
        for it, t in enumerate(tiles):
            kht, khb = nxt_prep
            if it + 1 < len(tiles):
                nxt_prep = prep_tile(tiles[it + 1])
            if nsub == 1:
                bank, bb = self.next_bank()
                self.mm(bank[:, 0:Wd], kht[:, 0:128], m["V"][:, t, 0:Wd], True, True, (khb, B["V"]), (bb,))
                dsl = [(bank[:, 0:Wd], bb)]
            else:
                dsl = []
                for g in range(nsub // 4):
                    bank, bb = self.next_bank()
                    for a4 in range(4):
                        a = g * 4 + a4
                        self.mm(bank[:, a4 * 128:(a4 + 1) * 128], kht[:, a * 128:(a + 1) * 128], m["V"][:, t, 0:128], True, True,
                                (khb, B["V"]), (bb,))
                        dsl.append((bank[:, a4 * 128:(a4 + 1) * 128], bb))
            subs = range(nsub) if d == 0 else range(nsub - 1, -1, -1)
            for a in subs:
                c = t * nsub + a
                nxt = c + 1 if d == 0 else c - 1
                if nxt < 0 or nxt >= nch:
                    continue
                cur = self.cur
                nx = 1 - cur
                dsa, dsb = dsl[a]
                inq = qt * qch <= nxt < (qt + 1) * qch
                if nsub > 1:
                    if inq:
                        self.stt(Sv(nxt), Sv(c), m["dec"][:, c:c + 1], dsa, ALU.mult, ALU.add, (B["S"], B["dec"], dsb), (B["S"],))
                    else:
                        self.stt(m["Sm"][nx][:, 0:Wd], Sv(c), m["dec"][:, c:c + 1], dsa, ALU.mult, ALU.add,
                                 (B["S"], B["dec"], dsb), (B["Sm"][nx],))
                        self.cur = nx
                    continue
                self.stt(m["Sm"][nx][:, 0:Wd], m["Sm"][cur][:, 0:Wd], m["dec"][:, c:c + 1], dsa,
                         ALU.mult, ALU.add, (B["Sm"][cur], B["dec"], dsb), (B["Sm"][nx],))
                if inq:
                    self.act(Sv(nxt), m["Sm"][nx][:, 0:Wd], AF.Copy, (B["Sm"][nx],), (B["S"],))
                self.cur = nx

    def outphase(self, d, nsub, Wd, hh, rows, vcol, qt):
        m = self.m
        B = dict(self.B)
        B["S"] = self.B["S"][qt % 2]
        C = 128 // nsub
        qch = (T // C) // 4
        mask = self.masks[(nsub, d)]
        S = m["S"][qt % 2]
        t4 = qt
        bankP, bPb = self.next_bank()
        for j in range(4):
            t = t4 * 4 + j
            ts_ = slice(t * 128, (t + 1) * 128)
            self.mm(bankP[:, j * 128:(j + 1) * 128], m["KT"][rows, ts_], m["QH"][rows, ts_], True, True,
                    (B["KT"], B["QH"]), (bPb,))
        pi = self.p_i
        self.p_i = 1 - pi
        Pt, Ptb = m["P"][pi], B["P"][pi]
        self.tt(Pt[:, :].rearrange("p (j n) -> p j n", n=128), bankP[:, :].rearrange("p (j n) -> p j n", n=128),
                mask.unsqueeze(1).to_broadcast([128, 4, 128]), ALU.mult, (bPb, self.cmb), (Ptb,))
        bankO, bOb = self.next_bank()
        for j in range(4):
            t = t4 * 4 + j
            self.mm(bankO[:, j * 128:(j + 1) * 128], m["V"][:, t, vcol:vcol + 128], Pt[:, j * 128:(j + 1) * 128], True, False,
                    (B["V"], Ptb), (bOb,))
            for a in range(nsub):
                c = t * nsub + a
                sl = c - qt * qch
                self.mm(bankO[:, j * 128 + a * C:j * 128 + (a + 1) * C],
                        S[rows, sl * Wd + hh * 128:sl * Wd + (hh + 1) * 128], m["QH"][rows, c * C:(c + 1) * C],
                        False, True, (B["S"], B["QH"]), (bOb,))
        osl = m["oT"][:, hh, t4 * 512:(t4 + 1) * 512]
        ob = B["oT"][hh][t4]
        if d == 0:
            self.act(osl, bankO[:, :], AF.Copy, (bOb,), (ob,))
        else:
            self.tt(osl, bankO[:, :], osl, ALU.add, (bOb, ob), (ob,))

    def scan_dir(self, d, nsub, Wd, heads):
        quarters = (0, 1, 2, 3) if d == 0 else (3, 2, 1, 0)
        self.chain(d, nsub, Wd, quarters[0], True)
        for iq, qt in enumerate(quarters):
            if iq + 1 < 4:
                self.chain(d, nsub, Wd, quarters[iq + 1], False)
            for (hh, rows, vcol) in heads:
                self.outphase(d, nsub, Wd, hh, rows, vcol, qt)

    @staticmethod
    def _merge_into(dst, srcs):
        for sb_ in srcs:
            for dct in (sb_.w, sb_.r):
                for k, ev in dct.items():
                    if k not in dst.w or dst.w[k][1] < ev[1]:
                        dst.w[k] = ev

    def yphase(self, s, li, hh, gate_name, gate_func, gain_ap, groupnorm, yidx):
        m, B = self.m, self.B
        tmp = m["tmp"]
        gt = [tmp[:, 0, 0:512], tmp[:, 0, 512:1024]]
        rs = [tmp[:, 1, 0:512], tmp[:, 1, 512:1024]]
        t1 = tmp[:, 2, 0:512]
        sqv = tmp[:, 2, 512:1024].bitcast(BF16)
        sq = [sqv[:, 0:512], sqv[:, 512:1024]]
        yst = m["yst"]
        f = {k: Buf(k) for k in ["gt0", "gt1", "rs0", "rs1", "t1", "sq0", "sq1"]}
        alias = {"gt0": 0, "gt1": 0, "rs0": 1, "rs1": 1, "t1": 2, "sq0": 2, "sq1": 2}
        for k, q in alias.items():
            self._merge_into(f[k], [B["tmp"][q]])
        gtb = [f["gt0"], f["gt1"]]
        rsb = [f["rs0"], f["rs1"]]
        sqb = [f["sq0"], f["sq1"]]
        t1b = f["t1"]
        wt, wb = self.load_w(self.d_wz[li, ZIDX[gate_name], :, :, :])
        for pr in range(2):
            blocks = [pr * 2, pr * 2 + 1]
            for i, tb in enumerate(blocks):
                cs = slice(tb * 512, (tb + 1) * 512)
                bank, bb = self.next_bank()
                for kc in range(KC):
                    self.mm(bank[:, :], wt[:, kc, :], self.hT[:, kc, cs], kc == 0, kc == KC - 1, (wb, self.hTb[kc][tb]), (bb,))
                self.act(gt[i], bank[:, :], gate_func, (bb,), (gtb[i],))
            banksN = []
            for i, tb in enumerate(blocks):
                cs = slice(tb * 512, (tb + 1) * 512)
                o = m["oT"][:, hh, cs]
                ob = B["oT"][hh][tb]
                if groupnorm:
                    self.vcopy(sq[i], o, (ob,), (sqb[i],))
                    bankM, bMb = self.next_bank()
                    self.mm(bankM[:, :], self.onesb, sq[i], True, True, (sqb[i], self.cmb), (bMb,))
                    self.stt(o, bankM[:, :], -1.0 / 128.0, o, ALU.mult, ALU.add, (bMb, ob), (ob,))
                self.tt(sq[i], o, o, ALU.mult, (ob,), (sqb[i],))
                bankN, bNb = self.next_bank()
                self.mm(bankN[:, :], self.onesb, sq[i], True, True, (sqb[i], self.cmb), (bNb,))
                banksN.append((bankN, bNb))
            for i, tb in enumerate(blocks):
                bankN, bNb = banksN[i]
                self.act(rs[i], bankN[:, :], AF.Ln, (bNb, self.smb), (rsb[i],), bias=self.epsc[:, 0:1], scale=1.0 / 128.0)
                self.act(rs[i], rs[i], AF.Exp, (rsb[i],), (rsb[i],), scale=-0.5)
            for i, tb in enumerate(blocks):
                cs = slice(tb * 512, (tb + 1) * 512)
                o = m["oT"][:, hh, cs]
                ob = B["oT"][hh][tb]
                self.stt(t1, o, gain_ap, rs[i], ALU.mult, ALU.mult, (ob, rsb[i], self.smb), (t1b,))
                self.tt(yst[:, cs], t1, gt[i], ALU.mult, (t1b, gtb[i]), (B["yst"],))
        self.dma(self.SP, self.d_yscr[yidx, :, :], yst[:, :], self.ds_y, (B["yst"],), (self.yscr_b[yidx],))
        for k, q in alias.items():
            self._merge_into(B["tmp"][q], [f[k]])

    def mixer_A(self, s, li, l, h):
        nc = self.nc
        m, B = self.m, self.B
        self.zgroup(li, f"QA{h}", lambda tb, cs, bank, bb: self.act(m["q"][:, cs], bank[:, :], AF.Silu, (bb,), (B["q"],)))
        self.vgroup(li, f"VA{h}", m["V"], B["V"], 0)
        if h == 0:
            self.dump("A_q", m["q"][:, :], (B["q"],), T)
        fs = m["oT"][:, 1, :]
        fb = B["oT"][1]
        lbc = self.lbv[:, l, h:h + 1]
        omc = self.omlv[:, l, h:h + 1]
        def prep(d):
            self.zgroup(li, f"FF{h}" if d == 0 else f"FB{h}",
                        lambda tb, cs, bank, bb: self.act(fs[:, cs], bank[:, :], AF.Sigmoid, (bb,), (fb[tb],)))
            self.ts(fs, fs, omc, lbc, ALU.mult, ALU.add, tuple(fb) + (self.cb,), tuple(fb))
            self.ts(m["k"][:, :], fs, -1.0, 1.0, ALU.mult, ALU.add, tuple(fb), (B["k"],))
            self.act(fs, fs, AF.Ln, tuple(fb), tuple(fb))
            self.op(self.DVE, lambda: nc.vector.tensor_tensor_scan(out=m["CUM"][:, 1:T + 1], data0=self.onec[:, 0:1].to_broadcast([128, T]),
                                                                    data1=fs, initial=0.0, op0=ALU.mult, op1=ALU.add),
                    tuple(fb) + (self.smb,), (B["CUM"],))

        prep(0)
        self.factors(0, 8, 1.0)
        prep(1)
        self.scan_dir(0, 8, 128, [(0, slice(0, 128), 0)])
        self.factors(1, 8, 1.0)
        self.scan_dir(1, 8, 128, [(0, slice(0, 128), 0)])
        o = SMO["nag"] + l * 4 + h
        self.yphase(s, li, 0, f"GA{h}", AF.Sigmoid, self.sm[:, o:o + 1], False, 0 + h)

    def rotary(self, li, dst, dstb, gname, gsname, scl):
        m, B = self.m, self.B
        ZA = m["tmp"][:, 0:2, :].rearrange("p a n -> p (a n)")
        ZB = m["tmp"][:, 2:4, :].rearrange("p a n -> p (a n)")
        bA = (B["tmp"][0], B["tmp"][1])
        bB = (B["tmp"][2], B["tmp"][3])
        self.zgroup(li, gname, lambda tb, cs, bank, bb: self.act(ZA[:, cs], bank[:, :], AF.Copy, (bb,), (bA[tb // 2],), scale=scl))
        self.zgroup(li, gsname, lambda tb, cs, bank, bb: self.act(ZB[:, cs], bank[:, :], AF.Copy, (bb,), (bB[tb // 2],), scale=scl))
        cosb = tuple(B["oT"][0])
        sinb = tuple(B["oT"][1])
        self.tt(ZA, ZA, m["oT"][:, 0, :], ALU.mult, bA + cosb, bA)
        self.tt(ZB, ZB, m["oT"][:, 1, :], ALU.mult, bB + sinb, bB)
        self.tt(dst[:, :], ZA, ZB, ALU.add, bA + bB, (dstb,))

    def mixer_B(self, s, li, l, P):
        m, B = self.m, self.B
        self.dma_multi(self.SP, [(m["oT"][:, 0, :], self.d_rot[0, :, :], (), tuple(B["oT"][0])),
                                 (m["oT"][:, 1, :], self.d_rot[1, :, :], (), tuple(B["oT"][1]))], self.ds_tab)
        self.rotary(li, m["q"], B["q"], f"QB{P}", f"QBs{P}", 1.0)
        self.rotary(li, m["k"], B["k"], f"KB{P}", f"KBs{P}", 0.125)
        self.vgroup(li, f"VB{2 * P}", m["V"], B["V"], 0)
        self.vgroup(li, f"VB{2 * P + 1}", m["V"], B["V"], 128)
        self.dma(self.SP, m["retf"][0][:, :], self.d_retf[P, 0, :, :], self.ds_retf[0], (), (B["retf"][0],))
        for d in range(2):
            tb_, tbb = m["retf"][d], B["retf"][d]
            v3 = lambda ap: ap.rearrange("p (c k) -> p c k", k=128)
            bc = lambda j: tb_[:, j * 128:(j + 1) * 128].unsqueeze(1).to_broadcast([128, NT, 128])
            self.tt(v3(m["QH"][:, :]), v3(m["q"][:, :]), bc(0), ALU.mult, (B["q"], tbb), (B["QH"],))
            self.tt(v3(m["KT"][:, :]), v3(m["k"][:, :]), bc(1), ALU.mult, (B["k"], tbb), (B["KT"],))
            self.tt(v3(m["KH"][:, :]), v3(m["k"][:, :]), bc(2), ALU.mult, (B["k"], tbb), (B["KH"],))
            self.vcopy(m["dec"][:, 0:NT], tb_[:, 384:384 + NT], (tbb,), (B["dec"],))
            if d == 0:
                self.dma(self.SP, m["retf"][1][:, :], self.d_retf[P, 1, :, :], self.ds_retf[0], (), (B["retf"][1],))
            self.scan_dir(d, 1, 256, [(hh, slice(hh * 64, hh * 64 + 64), hh * 128) for hh in range(2)])
        for hh in range(2):
            h = 2 * P + hh
            o = SMO["nbg"] + l * 4 + h
            self.yphase(s, li, hh, f"GB{h}", AF.Silu, self.sm[:, o:o + 1], True, 4 + h)

    def mixer_C(self, s, li, l, P):
        nc = self.nc
        m, B = self.m, self.B
        self.zgroup(li, f"QC{P}", lambda tb, cs, bank, bb: self.act(m["q"][:, cs], bank[:, :], AF.Copy, (bb,), (B["q"],), scale=0.125))
        self.zgroup(li, f"KC{P}", lambda tb, cs, bank, bb: self.act(m["k"][:, cs], bank[:, :], AF.Copy, (bb,), (B["k"],)))
        self.vgroup(li, f"VC{2 * P}", m["V"], B["V"], 0)
        self.vgroup(li, f"VC{2 * P + 1}", m["V"], B["V"], 128)
        X = m["tmp"][:, 0:2, :].rearrange("p a n -> p (a n)")
        bX = (B["tmp"][0], B["tmp"][1])
        for d in range(2):
            wt, wb = self.load_w(self.d_wz[li, ZIDX["LR"], :, :, :])
            for tb in range(NB):
                cs = slice(tb * 512, (tb + 1) * 512)
                bank, bb = self.next_bank()
                for kc in range(KC):
                    self.mm(bank[0:16, :], wt[:, kc, d * 16:(d + 1) * 16], self.hT[:, kc, cs], kc == 0, kc == KC - 1,
                            (wb, self.hTb[kc][tb]), (bb,))
                self.act(m["KH"][0:16, cs], bank[0:16, :], AF.Copy, (bb,), (B["KH"],))
            wo = d * 256 + P * 128
            nb_ = self.nbal[:, l * 4 + d * 2 + P:l * 4 + d * 2 + P + 1]
            for tb in range(NB):
                cs = slice(tb * 512, (tb + 1) * 512)
                bank, bb = self.next_bank()
                self.mm(bank[:, :], m["wal"][0:16, wo:wo + 128], m["KH"][0:16, cs], True, True, (B["wal"], B["KH"]), (bb,))
                self.act(X[:, cs], bank[:, :], AF.Exp, (bb, self.cb), (bX[tb // 2],), bias=nb_, scale=-1.0)
            self.act(X, X, AF.Ln, bX + (self.smb,), bX, bias=self.onec[:, 0:1], scale=1.0)
            self.op(self.DVE, lambda: nc.vector.tensor_tensor_scan(out=m["CUM"][:, 1:T + 1], data0=self.onec[:, 0:1].to_broadcast([128, T]),
                                                                    data1=X, initial=0.0, op0=ALU.mult, op1=ALU.add),
                    bX + (self.smb,), (B["CUM"],))
            self.factors(d, 1, -1.0 / 16.0)
            self.scan_dir(d, 1, 256, [(hh, slice(hh * 64, hh * 64 + 64), hh * 128) for hh in range(2)])
        for hh in range(2):
            h = 2 * P + hh
            o = SMO["ncg"] + l * 4 + h
            self.yphase(s, li, hh, f"GC{h}", AF.Silu, self.sm[:, o:o + 1], False, 8 + h)

    def merge_phase(self, s, li, l):
        with ExitStack() as es:
            W = lambda name, shape, dt: self.sb(es, name, shape, dt)
            yall = W("g_yall", [128, 12, T], BF16)
            yb = Buf()
            mg = W("g_mg", [128, KC, T], BF16)
            mgb = [Buf() for _ in range(NB)]
            gs = [W(f"g_gs{i}", [128, 512], F32) for i in range(2)]
            gsb = [Buf() for _ in range(2)]
            acc = W("g_acc", [128, T], F32)
            accb = [Buf() for _ in range(NB)]
            items = [(yall[:, i, :], self.d_yscr[i, :, :], (self.yscr_b[i],), (yb,)) for i in range(12)]
            self.dma_multi(self.SP, items, self.ds_yld)
            for i in range(12):
                self.dump(f"y{i}", yall[:, i, :], (yb,), T)
            mnames = ["GMa", "GMb", "GMc"]
            for oc in range(8):
                for mi in range(3):
                    wg = self.load_w(self.d_wz[li, ZIDX[f"{mnames[mi]}{oc}"], :, :, :])
                    wp = self.load_w(self.d_wp[li, mi, oc, :, :, :], nk=4)
                    for tb in range(NB):
                        cs = slice(tb * 512, (tb + 1) * 512)
                        gi = tb % 2
                        bankG, bGb = self.next_bank()
                        for kc in range(KC):
                            self.mm(bankG[:, :], wg[0][:, kc, :], self.hT[:, kc, cs], kc == 0, kc == KC - 1,
                                    (wg[1], self.hTb[kc][tb]), (bGb,))
                        self.act(gs[gi][:, :], bankG[:, :], AF.Sigmoid, (bGb,), (gsb[gi],))
                        bankP, bPb = self.next_bank()
                        for kc in range(4):
                            self.mm(bankP[:, :], wp[0][:, kc, :], yall[:, mi * 4 + kc, cs], kc == 0, kc == 3,
                                    (wp[1], yb), (bPb,))
                        if mi == 0:
                            self.tt(acc[:, cs], bankP[:, :], gs[gi][:, :], ALU.mult, (bPb, gsb[gi]), (accb[tb],))
                        elif mi == 1:
                            self.tt(gs[gi][:, :], bankP[:, :], gs[gi][:, :], ALU.mult, (bPb, gsb[gi]), (gsb[gi],))
                            self.tt(acc[:, cs], acc[:, cs], gs[gi][:, :], ALU.add, (accb[tb], gsb[gi]), (accb[tb],))
                        else:
                            self.tt(gs[gi][:, :], bankP[:, :], gs[gi][:, :], ALU.mult, (bPb, gsb[gi]), (gsb[gi],))
                            self.tt(mg[:, oc, cs], acc[:, cs], gs[gi][:, :], ALU.add, (accb[tb], gsb[gi]), (mgb[tb],))
            self.dump("mg0", mg[:, 0, :], tuple(mgb), T)
            for oc in range(8):
                wt, wb = self.load_w(self.d_wo[li, oc, :, :, :])
                g1 = self.modT[:, li, 16 + oc, s:s + 1]
                for tb in range(NB):
                    cs = slice(tb * 512, (tb + 1) * 512)
                    bank, bb = self.next_bank()
                    for kc in range(KC):
                        self.mm(bank[:, :], wt[:, kc, :], mg[:, kc, cs], kc == 0, kc == KC - 1, (wb, mgb[tb]), (bb,))
                    xb = self.xTb[oc][tb]
                    self.stt(self.xT[:, oc, cs], bank[:, :], g1, self.xT[:, oc, cs], ALU.mult, ALU.add, (bb, xb, self.modb), (xb,))
            self.barrier()

    def ffn_phase(self, s, li, l):
        with ExitStack() as es:
            W = lambda name, shape, dt: self.sb(es, name, shape, dt)
            with ExitStack() as nes:
                self.norm_to_h(nes, li, 1, s)
                self.barrier()
            actT = W("f_act", [128, 11, T], BF16)
            ab = [Buf() for _ in range(NB)]
            st = [W(f"f_st{i}", [128, 512], F32) for i in range(2)]
            stb = [Buf(), Buf()]
            w2 = [W(f"f_w2{i}", [128, 11, 128], BF16) for i in range(2)]
            w2b = [Buf(), Buf()]
            w2ds = self.ds_w2
            for gi in range(2):
                for fcl in range(11):
                    fc = gi * 11 + fcl
                    wg, wgb = self.load_w(self.d_wf1[li, fc, 0, :, :, :])
                    wu, wub = self.load_w(self.d_wf1[li, fc, 1, :, :, :])
                    for tb in range(NB):
                        cs = slice(tb * 512, (tb + 1) * 512)
                        i = tb % 2
                        bankG, bGb = self.next_bank()
                        for kc in range(KC):
                            self.mm(bankG[:, :], wg[:, kc, :], self.hT[:, kc, cs], kc == 0, kc == KC - 1, (wgb, self.hTb[kc][tb]), (bGb,))
                        bankU, bUb = self.next_bank()
                        for kc in range(KC):
                            self.mm(bankU[:, :], wu[:, kc, :], self.hT[:, kc, cs], kc == 0, kc == KC - 1, (wub, self.hTb[kc][tb]), (bUb,))
                        self.act(st[i][:, :], bankG[:, :], AF.Silu, (bGb,), (stb[i],))
                        self.tt(actT[:, fcl, cs], bankU[:, :], st[i][:, :], ALU.mult, (bUb, stb[i]), (ab[tb],))
                for oc in range(8):
                    i = oc % 2
                    self.dma(self.POOL, w2[i][:, :, :], self.d_wf2[li, gi, oc, :, :, :], w2ds[i], (), (w2b[i],))
                    g2 = self.modT[:, li, 40 + oc, s:s + 1]
                    for tb in range(NB):
                        cs = slice(tb * 512, (tb + 1) * 512)
                        bank, bb = self.next_bank()
                        for fcl in range(11):
                            self.mm(bank[:, :], w2[i][:, fcl, :], actT[:, fcl, cs], fcl == 0, fcl == 10, (w2b[i], ab[tb]), (bb,))
                        xb = self.xTb[oc][tb]
                        self.stt(self.xT[:, oc, cs], bank[:, :], g2, self.xT[:, oc, cs], ALU.mult, ALU.add, (bb, xb, self.modb), (xb,))
            self.barrier()

    def layer(self, s, li, l):
        self.mixer_phase(s, li, l)
        self.merge_phase(s, li, l)
        self.ffn_phase(s, li, l)

    def finish(self, s):
        with ExitStack() as es:
            W = lambda name, shape, dt: self.sb(es, name, shape, dt)
            ost = [W(f"o_st{i}", [128, 512], F32) for i in range(2)]
            ostb = [Buf(), Buf()]
            k = 0
            if self.final:
                sq = [W(f"osq{i}", [128, 512], BF16) for i in range(2)]
                rs = [W(f"ors{i}", [128, 512], F32) for i in range(2)]
                bufs = (sq, [Buf(), Buf()], rs, [Buf(), Buf()], None, None)
                o = SMO["nfg"]
                for tb, rs_, rsb in self.emit_norm(es, None, None, bufs):
                    cs = slice(tb * 512, (tb + 1) * 512)
                    for kc in range(KC):
                        i = k % 2
                        k += 1
                        self.stt(ost[i][:, :], self.xT[:, kc, cs], self.sm[:, o + kc:o + kc + 1], rs_[:, :], ALU.mult, ALU.mult,
                                 (self.xTb[kc][tb], rsb, self.smb), (ostb[i],))
                        self.dma(self.SP, self.d_out[s, :, kc, cs], ost[i][:, :], self.ds_out[i], (ostb[i],), ())
            else:
                for kc in range(KC):
                    self.dma(self.SP, self.d_out[s, :, kc, :], self.xT[:, kc, :], self.ds_out[kc % 2], tuple(self.xTb[kc]), ())
            self.barrier()


def _tile_w(w, nk):
    n = w.shape[1]
    return np.ascontiguousarray(w.reshape(nk, 128, n).transpose(1, 0, 2))


def _const_tables():
    j = np.arange(128)[:, None]
    i = np.arange(128)[None, :]
    same = (j // 16) == (i // 16)
    cm = np.zeros((128, 6, 128), np.float32)
    cm[:, 0, :] = np.eye(128)
    cm[:, 1, :] = 1.0
    cm[:, 2, :] = (j <= i)
    cm[:, 3, :] = (j > i)
    cm[:, 4, :] = (j <= i) & same
    cm[:, 5, :] = (j > i) & same
    pos = np.arange(T, dtype=np.float32)
    inv_freq = (np.float32(10000.0) ** (-np.arange(0, 64, 2, dtype=np.float32) / np.float32(64))).astype(np.float32)
    ang = (pos[:, None] * inv_freq[None, :]).astype(np.float32)
    cos = np.cos(ang).astype(np.float32).T
    sin = np.sin(ang).astype(np.float32).T
    rot = np.zeros((2, 128, T), np.float32)
    for p in range(128):
        d = p % 64
        rot[0, p] = cos[d % 32]
        rot[1, p] = -sin[d % 32] if d < 32 else sin[d % 32]
    cumB = np.zeros((2, 2, 128, T + 1), np.float32)
    tt = np.arange(T + 1, dtype=np.float64)
    for P in range(2):
        for dd in range(2):
            for p in range(128):
                h = 2 * P + p // 64
                hx = h if dd == 0 else 3 - h
                lg = np.log1p(-np.exp2(-5.0 - hx))
                cumB[P, dd, p] = (tt * lg).astype(np.float32)
    retf = np.zeros((2, 2, 128, 3 * 128 + 16), np.float32)
    kk = np.arange(128, dtype=np.float64)
    for P in range(2):
        for dd in range(2):
            for p in range(128):
                h = 2 * P + p // 64
                hx = h if dd == 0 else 3 - h
                lg = np.log1p(-np.exp2(-5.0 - hx))
                if dd == 0:
                    eq, ekh = np.exp((kk + 1) * lg), np.exp((127 - kk) * lg)
                    ekt = np.exp(-(kk + 1) * lg)
                else:
                    eq, ekh = np.exp((128 - kk) * lg), np.exp(kk * lg)
                    ekt = np.exp(-(128 - kk) * lg)
                retf[P, dd, p, 0:128] = eq
                retf[P, dd, p, 128:256] = ekt
                retf[P, dd, p, 256:384] = ekh
                retf[P, dd, p, 384:400] = np.exp(128 * lg)
    return cm, rot, cumB, retf


def _prep_layer_weights(inp, l):
    w_in = inp["w_in"][l]
    wz = np.stack([_tile_w(w_in[:, ZG[n]], 8) for n in ZNAMES], 0)
    wps = [inp["w_pa"][l], inp["w_pb"][l], inp["w_pc"][l]]
    wp = np.stack([np.stack([_tile_w(w[:, oc * 128:(oc + 1) * 128], 4) for oc in range(8)], 0) for w in wps], 0)
    wo = np.stack([_tile_w(inp["w_out"][l][:, oc * 128:(oc + 1) * 128], 8) for oc in range(8)], 0)
    wfi = inp["w_ffn_in"][l]
    wf1 = np.stack([np.stack([_tile_w(wfi[:, fc * 128:(fc + 1) * 128], 8),
                              _tile_w(wfi[:, DFF + fc * 128:DFF + (fc + 1) * 128], 8)], 0) for fc in range(NFC)], 0)
    wfo = inp["w_ffn_out"][l]
    wf2 = np.stack([np.stack([_tile_w(wfo[gi * 1408:(gi + 1) * 1408, oc * 128:(oc + 1) * 128], 11) for oc in range(8)], 0)
                    for gi in range(2)], 0)
    wada = np.stack([_tile_w(inp["w_ada"][l][:, g * 384:(g + 1) * 384], 8) for g in range(16)], 0)
    return dict(wz=wz, wp=wp, wo=wo, wf1=wf1, wf2=wf2, wada=wada)


def _smalls(inp, c2):
    sm = np.zeros((128, NSM), np.float32)
    col = lambda v, n: np.ascontiguousarray(v.reshape(n, 128).T)
    sm[:, SMO["c"]:SMO["c"] + 16] = np.stack([col(c2[b], 8) for b in range(2)], 2).reshape(128, 16)
    for l in range(L):
        sm[:, SMO["bada"] + l * 48:SMO["bada"] + (l + 1) * 48] = col(inp["b_ada"][l], 48)
        sm[:, SMO["n1g"] + l * 8:SMO["n1g"] + (l + 1) * 8] = col(inp["norm1_g"][l], 8)
        sm[:, SMO["n2g"] + l * 8:SMO["n2g"] + (l + 1) * 8] = col(inp["norm2_g"][l], 8)
        sm[:, SMO["nag"] + l * 4:SMO["nag"] + (l + 1) * 4] = col(inp["norm_a_g"][l], 4)
        sm[:, SMO["nbg"] + l * 4:SMO["nbg"] + (l + 1) * 4] = col(inp["norm_b_g"][l], 4)
        sm[:, SMO["ncg"] + l * 4:SMO["ncg"] + (l + 1) * 4] = col(inp["norm_c_g"][l], 4)
        sm[:, SMO["lbl"] + l * 4:SMO["lbl"] + (l + 1) * 4] = col(inp["lb_logits"][l], 4)
        for d in range(2):
            sm[:, SMO["bal"] + l * 4 + d * 2:SMO["bal"] + l * 4 + d * 2 + 2] = col(inp["b_alpha"][l, d], 2)
    sm[:, SMO["nfg"]:SMO["nfg"] + 8] = col(inp["norm_f_g"], 8)
    for a in range(8):
        sm[a * 16:(a + 1) * 16, SMO["bmask"] + a] = 1.0
    return sm


_CACHE = {}


def _get_nc(layers, final):
    key = (tuple(layers), final)
    if key not in _CACHE:
        _CACHE[key] = Builder(list(layers), final).build()
    return _CACHE[key]


def _run(inp, xT_cores, layers, final):
    nc = _get_nc(layers, final)
    cm, rot, cumB, retf = _const_tables()
    lw = [_prep_layer_weights(inp, l) for l in layers]
    shared = {k: np.stack([w[k] for w in lw], 0) for k in lw[0]}
    wal = np.ascontiguousarray(inp["w_alpha"].transpose(2, 0, 1, 3))
    in_maps = []
    for core in range(NCORES):
        c2 = inp["c"][2 * core:2 * core + 2]
        mp = dict(xT=xT_cores[core], smalls=_smalls(inp, c2), wal=wal, cmat=cm, rot=rot, cumB=cumB, retf=retf)
        mp.update(shared)
        in_maps.append(mp)
    res = run_bass_kernel_spmd(nc, in_maps, core_ids=list(range(NCORES)))
    return [np.asarray(r["yT"]) for r in res.results]


def _to_xT(x):
    out = []
    for core in range(NCORES):
        xs = x[2 * core:2 * core + 2]
        out.append(np.ascontiguousarray(xs.transpose(0, 2, 1).reshape(2, KC, 128, T).transpose(0, 2, 1, 3)))
    return out


def _from_xT(outs):
    res = np.empty((16, T, D), np.float32)
    for core in range(NCORES):
        o = outs[core]
        res[2 * core:2 * core + 2] = o.transpose(0, 2, 1, 3).reshape(2, D, T).transpose(0, 2, 1)
    return res


FUSED = True


def kernel(**inputs):
    inp = {k: np.asarray(v, dtype=np.float32) for k, v in inputs.items()}
    xT = _to_xT(inp["x"])
    if FUSED:
        outs = _run(inp, xT, [0, 1, 2, 3], True)
    else:
        outs = xT
        for l in range(L):
            outs = _run(inp, outs, [l], l == L - 1)
    return _from_xT(outs)
```
